# Optimizing a Trainium2 kernel written in Bass

```python
import jax, jax.numpy as jnp
from jax import lax
import numpy as np

D_MODEL = 2048
BATCH = 1
SEQ = 8192
DEPTH = 4

N_MEM = 256
XA_HEADS = 4
XA_HEAD_DIM = D_MODEL // XA_HEADS
GLA_HEADS = 4
GLA_DK = 64
GLA_DV = 128
GLA_GATE_RANK = 16
GLA_GATE_TAU = 16.0
GLA_CHUNK = 64
SWA_Q_HEADS = 16
SWA_KV_HEADS = 2
SWA_HEAD_DIM = 64
SWA_WINDOW = 128
ROPE_THETA = 10000.0
SC_CH = 512
SC_WIDTH = 3
D_MIX = GLA_HEADS * GLA_DV + SWA_Q_HEADS * SWA_HEAD_DIM + SC_CH
IN_SIZES = (
    GLA_HEADS * GLA_DK, GLA_HEADS * GLA_DK, GLA_HEADS * GLA_DV, GLA_HEADS * GLA_DV, GLA_GATE_RANK,
    SWA_Q_HEADS * SWA_HEAD_DIM, SWA_KV_HEADS * SWA_HEAD_DIM, SWA_KV_HEADS * SWA_HEAD_DIM,
    SC_CH, SC_CH, SC_CH,
)
N_IN = sum(IN_SIZES)
D_FF = 5632
FFN_CONV_WIDTH = 3
EPS = 1e-6

kernel_name = "hybrid_gla_swa_shortconv_trunk"


def rms_norm(x, g):
    xf = x.astype(jnp.float32)
    y = xf * lax.rsqrt(jnp.mean(xf * xf, axis=-1, keepdims=True) + EPS)
    return (y * g.astype(jnp.float32)).astype(x.dtype)


def split_cols(z, sizes):
    return jnp.split(z, [int(i) for i in np.cumsum(sizes)[:-1]], axis=-1)


def rope_tables(positions, dim):
    inv = 1.0 / (ROPE_THETA ** (jnp.arange(0, dim, 2, dtype=jnp.float32) / dim))
    ang = positions.astype(jnp.float32)[..., None] * inv
    return jnp.cos(ang), jnp.sin(ang)


def apply_rope(x, cos, sin):
    xf = x.astype(jnp.float32)
    x1, x2 = jnp.split(xf, 2, axis=-1)
    c, s = cos[:, :, None, :], sin[:, :, None, :]
    return jnp.concatenate([x1 * c - x2 * s, x2 * c + x1 * s], axis=-1)


def causal_dwconv(u, w):
    K = w.shape[0]
    T = u.shape[1]
    up = jnp.pad(u, ((0, 0), (K - 1, 0), (0, 0)))
    y = up[:, 0:T] * w[0]
    for j in range(1, K):
        y = y + up[:, j:j + T] * w[j]
    return y


def gla_chunked(q, k, v, log_a):
    B, T, H, dk = q.shape
    dv = v.shape[-1]
    C = GLA_CHUNK
    NC = T // C

    def to_chunks(a):
        return a.astype(jnp.float32).reshape(B, NC, C, H, a.shape[-1]).transpose(1, 0, 3, 2, 4)

    qc, kc, vc, gc = to_chunks(q * (dk ** -0.5)), to_chunks(k), to_chunks(v), to_chunks(log_a)
    mask = jnp.tril(jnp.ones((C, C), dtype=bool))[:, :, None]

    def step(S, inp):
        qi, ki, vi, gi = inp
        b = jnp.cumsum(gi, axis=-2)
        b_last = b[..., -1:, :]
        o_inter = jnp.einsum('bhcd,bhde->bhce', qi * jnp.exp(b), S)
        diff = b[..., :, None, :] - b[..., None, :, :]
        decay = jnp.exp(jnp.where(mask, diff, -jnp.inf))
        att = jnp.einsum('bhtd,bhsd,bhtsd->bhts', qi, ki, decay)
        o_intra = jnp.einsum('bhts,bhse->bhte', att, vi)
        S_new = S * jnp.exp(b_last)[:, :, 0, :, None] + jnp.einsum(
            'bhsd,bhse->bhde', ki * jnp.exp(b_last - b), vi)
        return S_new, o_inter + o_intra

    S0 = jnp.zeros((B, H, dk, dv), jnp.float32)
    _, out = lax.scan(step, S0, (qc, kc, vc, gc))
    return out.transpose(1, 0, 3, 2, 4).reshape(B, T, H, dv)


def swa_with_sinks(q, k, v, sinks):
    B, T, Hq, hd = q.shape
    Hkv = k.shape[2]
    G = Hq // Hkv
    W = SWA_WINDOW
    NB = T // W
    vf = v.astype(jnp.float32)
    qb = q.reshape(B, NB, W, Hkv, G, hd)

    def with_prev(a):
        ab = a.reshape(B, NB, W, Hkv, hd)
        prev = jnp.concatenate([jnp.zeros_like(ab[:, :1]), ab[:, :-1]], axis=1)
        return jnp.concatenate([prev, ab], axis=2)

    kk, vv = with_prev(k), with_prev(vf)
    s = jnp.einsum('bnqhgd,bnkhd->bnhgqk', qb, kk) * (hd ** -0.5)
    qi = jnp.arange(W)[:, None] + W
    ki = jnp.arange(2 * W)[None, :]
    rel = qi - ki
    allowed = (rel >= 0) & (rel < W)
    blk = jnp.arange(NB)[:, None, None]
    valid = allowed[None] & ((blk > 0) | (ki[None] >= W))
    s = jnp.where(valid[None, :, None, None], s, -jnp.inf)
    sink = sinks.astype(jnp.float32).reshape(Hkv, G)[None, None, :, :, None, None]
    m = jnp.maximum(jnp.max(s, axis=-1, keepdims=True), sink)
    p = jnp.exp(s - m)
    denom = jnp.sum(p, axis=-1, keepdims=True) + jnp.exp(sink - m)
    o = jnp.einsum('bnhgqk,bnkhd->bnqhgd', p / denom, vv)
    return o.reshape(B, T, Hq * hd)


def hybrid_mixer(h, cos, sin, w_in, gla_w_gate, gla_b_gate, gla_norm, swa_sinks, sc_conv, w_out):
    B, T, _ = h.shape
    z = h @ w_in
    (g_q, g_k, g_v, g_r, g_lr, s_q, s_k, s_v, c_b, c_c, c_h) = split_cols(z, IN_SIZES)
    log_a = jax.nn.log_sigmoid((g_lr @ gla_w_gate + gla_b_gate).astype(jnp.float32)) / GLA_GATE_TAU
    o_a = gla_chunked(g_q.reshape(B, T, GLA_HEADS, GLA_DK), g_k.reshape(B, T, GLA_HEADS, GLA_DK),
                      g_v.reshape(B, T, GLA_HEADS, GLA_DV), log_a.reshape(B, T, GLA_HEADS, GLA_DK))
    o_a = rms_norm(o_a, gla_norm).reshape(B, T, GLA_HEADS * GLA_DV)
    o_a = (o_a * jax.nn.silu(g_r.astype(jnp.float32))).astype(h.dtype)
    q = apply_rope(s_q.reshape(B, T, SWA_Q_HEADS, SWA_HEAD_DIM), cos, sin)
    k = apply_rope(s_k.reshape(B, T, SWA_KV_HEADS, SWA_HEAD_DIM), cos, sin)
    o_b = swa_with_sinks(q, k, s_v.reshape(B, T, SWA_KV_HEADS, SWA_HEAD_DIM), swa_sinks).astype(h.dtype)
    o_c = c_b * causal_dwconv(c_c * c_h, sc_conv)
    return jnp.concatenate([o_a, o_b, o_c.astype(h.dtype)], axis=-1) @ w_out


def memory_cross_attention(h, m, wq, wk, wv, wo):
    B, T, _ = h.shape
    q = (h @ wq).reshape(B, T, XA_HEADS, XA_HEAD_DIM).astype(jnp.float32)
    k = (m @ wk).reshape(B, N_MEM, XA_HEADS, XA_HEAD_DIM).astype(jnp.float32)
    v = (m @ wv).reshape(B, N_MEM, XA_HEADS, XA_HEAD_DIM).astype(jnp.float32)
    p = jax.nn.softmax(jnp.einsum('bthd,bmhd->bhtm', q, k) * (XA_HEAD_DIM ** -0.5), axis=-1)
    o = jnp.einsum('bhtm,bmhd->bthd', p, v).reshape(B, T, D_MODEL).astype(h.dtype)
    return o @ wo


def conv_ffn(h, w_up, conv_w, conv_b, w_down):
    u = causal_dwconv(h @ w_up, conv_w) + conv_b
    g, val = jnp.split(u, 2, axis=-1)
    return (jax.nn.silu(g) * val) @ w_down


def setup_inputs(seed: int = 0) -> dict:
    key = jax.random.key(seed)
    ks = jax.random.split(key, 24)
    f32 = jnp.float32

    def nrm(k, shape, scale):
        return jax.random.normal(k, shape, f32) * scale

    def gain(k, shape):
        return 1.0 + 0.02 * jax.random.normal(k, shape, f32)

    offset = jax.random.randint(ks[2], (BATCH, 1), 0, 1024, dtype=jnp.int32)
    positions = offset + jnp.arange(SEQ, dtype=jnp.int32)[None, :]
    return {
        "x": nrm(ks[0], (BATCH, SEQ, D_MODEL), 1.0),
        "mem": nrm(ks[1], (BATCH, N_MEM, D_MODEL), 1.0),
        "positions": positions,
        "norm_mix": gain(ks[3], (DEPTH, D_MODEL)),
        "w_in": nrm(ks[4], (DEPTH, D_MODEL, N_IN), D_MODEL ** -0.5),
        "gla_w_gate": nrm(ks[5], (DEPTH, GLA_GATE_RANK, GLA_HEADS * GLA_DK), GLA_GATE_RANK ** -0.5),
        "gla_b_gate": nrm(ks[6], (DEPTH, GLA_HEADS * GLA_DK), 0.1),
        "gla_norm": gain(ks[7], (DEPTH, GLA_DV)),
        "swa_sinks": nrm(ks[8], (DEPTH, SWA_Q_HEADS), 0.5),
        "sc_conv": nrm(ks[9], (DEPTH, SC_WIDTH, SC_CH), SC_WIDTH ** -0.5),
        "w_out": nrm(ks[10], (DEPTH, D_MIX, D_MODEL), D_MIX ** -0.5),
        "norm_x": gain(ks[11], (DEPTH, D_MODEL)),
        "norm_mem": gain(ks[12], (DEPTH, D_MODEL)),
        "xa_wq": nrm(ks[13], (DEPTH, D_MODEL, D_MODEL), D_MODEL ** -0.5),
        "xa_wk": nrm(ks[14], (DEPTH, D_MODEL, D_MODEL), D_MODEL ** -0.5),
        "xa_wv": nrm(ks[15], (DEPTH, D_MODEL, D_MODEL), D_MODEL ** -0.5),
        "xa_wo": nrm(ks[16], (DEPTH, D_MODEL, D_MODEL), D_MODEL ** -0.5),
        "norm_ffn": gain(ks[17], (DEPTH, D_MODEL)),
        "ffn_w_up": nrm(ks[18], (DEPTH, D_MODEL, 2 * D_FF), D_MODEL ** -0.5),
        "ffn_conv": nrm(ks[19], (DEPTH, FFN_CONV_WIDTH, 2 * D_FF), FFN_CONV_WIDTH ** -0.5),
        "ffn_conv_b": nrm(ks[20], (DEPTH, 2 * D_FF), 0.01),
        "ffn_w_down": nrm(ks[21], (DEPTH, D_FF, D_MODEL), D_FF ** -0.5),
        "norm_final": gain(ks[22], (D_MODEL,)),
    }


def reference(x, mem, positions, norm_mix, w_in, gla_w_gate, gla_b_gate, gla_norm, swa_sinks, sc_conv,
              w_out, norm_x, norm_mem, xa_wq, xa_wk, xa_wv, xa_wo, norm_ffn, ffn_w_up, ffn_conv,
              ffn_conv_b, ffn_w_down, norm_final):
    cos, sin = rope_tables(positions, SWA_HEAD_DIM)
    h = x
    for l in range(DEPTH):
        h = h + hybrid_mixer(rms_norm(h, norm_mix[l]), cos, sin, w_in[l], gla_w_gate[l], gla_b_gate[l],
                             gla_norm[l], swa_sinks[l], sc_conv[l], w_out[l])
        h = h + memory_cross_attention(rms_norm(h, norm_x[l]), rms_norm(mem, norm_mem[l]),
                                       xa_wq[l], xa_wk[l], xa_wv[l], xa_wo[l])
        h = h + conv_ffn(rms_norm(h, norm_ffn[l]), ffn_w_up[l], ffn_conv[l], ffn_conv_b[l], ffn_w_down[l])
    return rms_norm(h, norm_final)
```

```python
import contextlib
import math
import numpy as np
import concourse.bass as bass
import concourse.mybir as mybir
from concourse.bass_utils import run_bass_kernel_spmd

F32 = mybir.dt.float32
BF16 = mybir.dt.bfloat16
I32 = mybir.dt.int32
AF = mybir.ActivationFunctionType
ALU = mybir.AluOpType

D = 2048
T = 1024
NCORE = 8
DEPTH = 4
NIN = 4368
DFF = 5632
NMEM = 256
EPS = 1e-6
NSLOT = 3
WCOLS = 256
SAME_SYNC = True
DOUBLE_CC = True

C_IDENT, C_TRI, C_LE, C_GT, C_ROTP, C_ONES = 0, 128, 256, 384, 512, 640
C_INVF, C_SEL, C_LT, C_HASPREV, C_GFIN = 768, 769, 777, 785, 786
NCST = 802
P_GMIX, P_GX, P_GMEM, P_GFFN, P_GLAN, P_SCONV, P_FCONV, P_FB, P_SINK = 0, 16, 32, 48, 64, 65, 77, 341, 429
NPP = 445
O_GQ, O_GK, O_GV, O_GR, O_LR, O_SQ, O_SK, O_SV, O_CB, O_CC, O_CH = 0, 256, 512, 1024, 1536, 1552, 2576, 2704, 2832, 3344, 3856


class Buf:
    __slots__ = ("name", "w", "r", "excl")

    def __init__(self, name, excl=False):
        self.name = name
        self.w = None
        self.r = {}
        self.excl = excl


class KB:
    def __init__(self, nc, es):
        self.nc = nc
        self.es = es
        self.eng = {"pe": nc.tensor, "act": nc.scalar, "dve": nc.vector, "pool": nc.gpsimd, "sp": nc.sync}
        self.sem = {e: es.enter_context(nc.semaphore("c_" + e)) for e in ("pe", "act", "dve", "pool")}
        self.cnt = {e: 0 for e in self.sem}
        self.waited = {}
        self.dsems = {}
        self.ps = []
        self.psb = []
        self.psi = 0

    def _deps(self, eng, reads, writes):
        deps = {}

        def add(ev):
            if ev is None:
                return
            key, h, v = ev
            if key not in deps or deps[key][1] < v:
                deps[key] = (h, v)

        for b in reads:
            add(b.w)
            if b.excl:
                for ev in b.r.values():
                    add(ev)
        for b in writes:
            add(b.w)
            for ev in b.r.values():
                add(ev)
        e = self.eng[eng]
        for key, (h, v) in deps.items():
            if key == eng and (eng == "pe" or not SAME_SYNC):
                continue
            if self.waited.get((eng, key), 0) >= v:
                continue
            e.wait_ge(h, v)
            self.waited[(eng, key)] = v

    def _post(self, ev, reads, writes):
        for b in reads:
            old = b.r.get(ev[0])
            if old is None or old[2] < ev[2]:
                b.r[ev[0]] = ev
        for b in writes:
            b.w = ev
            b.r = {}

    def emit(self, eng, fn, reads=(), writes=(), inc=True):
        self._deps(eng, reads, writes)
        inst = fn(self.eng[eng])
        if inc:
            self.cnt[eng] += 1
            inst.then_inc(self.sem[eng], 1)
            ev = (eng, self.sem[eng], self.cnt[eng])
        else:
            ev = (eng, self.sem[eng], self.cnt[eng] + 1)
        self._post(ev, reads, writes)
        return inst

    def dma(self, q, out, in_, reads=(), writes=(), sem="d"):
        self._deps(q, reads, writes)
        if sem not in self.dsems:
            self.dsems[sem] = [self.es.enter_context(self.nc.semaphore("d_" + sem)), 0]
        ent = self.dsems[sem]
        ent[1] += 16
        self.eng[q].dma_start(out=out, in_=in_).then_inc(ent[0], 16)
        ev = ("dma:" + sem, ent[0], ent[1])
        self._post(ev, reads, writes)

    def mm(self, out, lhsT, rhs, start, stop, reads, writes, inc=True):
        return self.emit("pe", lambda e: e.matmul(out, lhsT=lhsT, rhs=rhs, start=start, stop=stop), reads, writes, inc)

    def next_ps(self):
        i = self.psi % len(self.ps)
        self.psi += 1
        return self.ps[i], self.psb[i]

    def barrier(self):
        for e in ("pe", "act", "dve", "pool", "sp"):
            for k in ("pe", "act", "dve"):
                if k == e:
                    continue
                v = self.cnt[k]
                if v > 0 and self.waited.get((e, k), 0) < v:
                    self.eng[e].wait_ge(self.sem[k], v)
                    self.waited[(e, k)] = v


CUT = None


def needed_weights(NL, stop_after):
    if NL == 0:
        return []
    n = ["w_in", "w_out"]
    if stop_after == "mixer" and NL == 1:
        return n
    n += ["xa_wq", "xa_wk", "xa_wv", "xa_wo"]
    if stop_after == "xa" and NL == 1:
        return n
    return n + ["ffn_w_up", "ffn_w_down"]


def build(NL=DEPTH, dbg_spec=None, stop_after=None):
    nc = bass.Bass("TRN2", target_bir_lowering=False)

    def din(name, shape, d=F32):
        return nc.dram_tensor(name, shape, d, kind="ExternalInput").ap()

    x_d = din("x", [T, D])
    mem_d = din("mem", [NMEM, D])
    pos_d = din("pos", [128, T], I32)
    cst_d = din("cst", [128, NCST])
    pp_d = din("pp", [DEPTH, 128, NPP])
    wg_d = din("wg", [DEPTH, 17, 256])
    need = needed_weights(NL, stop_after)
    w_in = din("w_in", [NL, D, NIN]) if "w_in" in need else None
    w_out = din("w_out", [NL, D, D]) if "w_out" in need else None
    wq_d = din("xa_wq", [NL, D, D]) if "xa_wq" in need else None
    wk_d = din("xa_wk", [NL, D, D]) if "xa_wk" in need else None
    wv_d = din("xa_wv", [NL, D, D]) if "xa_wv" in need else None
    wo_d = din("xa_wo", [NL, D, D]) if "xa_wo" in need else None
    wup_d = din("ffn_w_up", [NL, D, 2 * DFF]) if "ffn_w_up" in need else None
    wdn_d = din("ffn_w_down", [NL, DFF, D]) if "ffn_w_down" in need else None
    out_d = nc.dram_tensor("out", [T, D], F32, kind="ExternalOutput").ap()
    gin1 = nc.dram_tensor("gin1", [128, 514], F32)
    gout1 = nc.dram_tensor("gout1", [1024, 514], F32)
    gin2 = nc.dram_tensor("gin2", [128, 264], F32)
    gout2 = nc.dram_tensor("gout2", [1024, 264], F32)
    gin3 = nc.dram_tensor("gin3", [128, 32], F32)
    gout3 = nc.dram_tensor("gout3", [1024, 32], F32)
    gb = {n: Buf(n) for n in ("gin1", "gout1", "gin2", "gout2", "gin3", "gout3")}
    dbg_d = None
    if dbg_spec:
        dbg_d = nc.dram_tensor("dbg", [128, dbg_spec], F32, kind="ExternalOutput").ap()
    dbg_off = [0]
    dbg_map = {}

    es = contextlib.ExitStack()
    with es:
        kb = KB(nc, es)

        uid = [0]

        def sb(stack, name, shape, d):
            uid[0] += 1
            return stack.enter_context(nc.sbuf_tensor("%s_u%d" % (name, uid[0]), shape, d))

        for i in range(8):
            kb.ps.append(es.enter_context(nc.psum_tensor("ps%d" % i, [128, 512], F32)))
            kb.psb.append(Buf("ps%d" % i, excl=True))

        hT = sb(es, "hT", [128, 16, T], F32)
        hTb = [Buf("hT%d" % k) for k in range(16)]
        cst = sb(es, "cst", [128, NCST], F32)
        cstb = Buf("cst")
        ones_bf = sb(es, "ones_bf", [128, 128], BF16)
        ident_bf = sb(es, "ident_bf", [128, 128], BF16)
        le_bf = sb(es, "le_bf", [128, 128], BF16)
        gt_bf = sb(es, "gt_bf", [128, 128], BF16)
        cbf = Buf("cbf")
        cosT = sb(es, "cosT", [128, T], BF16)
        sinT = sb(es, "sinT", [128, T], BF16)
        csb = Buf("cossin")
        pp = sb(es, "pp", [128, NPP], F32)
        ppb = Buf("pp")
        wgf = sb(es, "wgf", [17, 256], F32)
        wgb = sb(es, "wgb", [17, 256], BF16)
        wgbuf = Buf("wg")
        wslots = [sb(es, "wsl%d" % i, [128, 16, WCOLS], BF16) for i in range(NSLOT)]
        wslotb = [Buf("wsl%d" % i) for i in range(NSLOT)]

        ident = cst[:, C_IDENT:C_IDENT + 128]
        tri = cst[:, C_TRI:C_TRI + 128]
        rotP = cst[:, C_ROTP:C_ROTP + 128]
        ones_f = cst[:, C_ONES:C_ONES + 128]

        def dump(name, ap, bufs, ncols):
            if dbg_d is None:
                return
            o = dbg_off[0]
            dbg_map[name] = (o, ncols, ap.shape[0])
            kb.dma("pool", dbg_d[0:ap.shape[0], o:o + ncols], ap, reads=bufs, writes=[], sem="dbg")
            dbg_off[0] += ncols

        steps = []
        wstate = {"issued": 0, "consumed": 0, "jobs": []}

        def wsrc(W, l, k0, nk, c0, ncol):
            return W[l, k0 * 128:(k0 + nk) * 128, c0:c0 + ncol].rearrange("(k p) n -> p k n", p=128)

        def add_job(parts, handler):
            steps.append(("job", parts, handler))

        def add_call(fn):
            steps.append(("call", fn))

        def issue_job(ji):
            parts = wstate["jobs"][ji]
            si = ji % NSLOT
            for part in parts:
                k0, nk, c0, ncol, src = part[:5]
                if len(part) > 5:
                    p0, pn = part[5], part[6]
                    kb.dma("pool", wslots[si][p0:p0 + pn, k0, c0:c0 + ncol], src, reads=[], writes=[wslotb[si]], sem="w%d" % si)
                else:
                    kb.dma("pool", wslots[si][:, k0:k0 + nk, c0:c0 + ncol], src, reads=[], writes=[wslotb[si]], sem="w%d" % si)

        def get_slot():
            idx = wstate["consumed"]
            wstate["consumed"] += 1
            while wstate["issued"] < min(len(wstate["jobs"]), idx + NSLOT):
                issue_job(wstate["issued"])
                wstate["issued"] += 1
            return wslots[idx % NSLOT], wslotb[idx % NSLOT]

        st = {}

        def act_copy(out, in_, reads, writes, scale=None):
            if scale is None:
                kb.emit("act", lambda e: e.activation(out=out, in_=in_, func=AF.Copy), reads, writes)
            else:
                kb.emit("act", lambda e: e.activation(out=out, in_=in_, func=AF.Copy, scale=scale), reads, writes)

        def rmsnorm(gcol0, dst, dstb, dst_off, sc):
            sq = [sb(sc, "sq%d" % i, [128, 512], BF16) for i in range(2)]
            sqb = [Buf("sq%d" % i) for i in range(2)]
            rs = sb(sc, "rs", [128, 512], F32)
            rsb = Buf("rs")
            for tt in range(2):
                ts = slice(tt * 512, (tt + 1) * 512)
                ps, pb = kb.next_ps()
                for k in range(16):
                    kb.emit("act", lambda e: e.activation(out=sq[k % 2][:], in_=hT[:, k, ts], func=AF.Square), [hTb[k]], [sqb[k % 2]])
                    kb.mm(ps[:], ones_bf[:], sq[k % 2][:], k == 0, k == 15, [sqb[k % 2], cbf], [pb])
                kb.emit("dve", lambda e: e.tensor_scalar(out=rs[:], in0=ps[:], scalar1=1.0 / D, scalar2=EPS, op0=ALU.mult, op1=ALU.add), [pb], [rsb])
                kb.emit("act", lambda e: e.activation(out=rs[:], in_=rs[:], func=AF.Sqrt), [rsb], [rsb])
                kb.emit("dve", lambda e: e.reciprocal(out=rs[:], in_=rs[:]), [rsb], [rsb])
                for k in range(16):
                    kb.emit("dve", lambda e: e.scalar_tensor_tensor(out=dst[:, k, dst_off + tt * 512:dst_off + (tt + 1) * 512], in0=hT[:, k, ts],
                                                                   scalar=pp[:, gcol0 + k:gcol0 + k + 1], in1=rs[:], op0=ALU.mult, op1=ALU.mult),
                            [hTb[k], rsb, ppb], [dstb[k]])

        def proj_fm(slot, slotb, ccols, rhs_fn, nk, epilogue, ntt=2):
            for ci, (c0, m) in enumerate(ccols):
                for tt in range(ntt):
                    ps, pb = kb.next_ps()
                    for k in range(nk):
                        r, rb = rhs_fn(k, tt)
                        kb.mm(ps[0:m, 0:r.shape[-1]], slot[:, k, c0:c0 + m], r, k == 0, k == nk - 1, [slotb, rb], [pb], inc=(k == nk - 1))
                    epilogue(ci, tt, ps, pb)

        def proj_tm(slot, slotb, c0, ncols, lhs_fn, nk, epilogue, ntb=8):
            for tb in range(ntb):
                ps, pb = kb.next_ps()
                for k in range(nk):
                    l, lb = lhs_fn(k, tb)
                    kb.mm(ps[0:l.shape[-1], 0:ncols], l, slot[:, k, c0:c0 + ncols], k == 0, k == nk - 1, [slotb, lb], [pb], inc=(k == nk - 1))
                epilogue(tb, ps, pb)

        gdi = nc.dram_tensor("gdi", [16, 16], F32)
        gdo = nc.dram_tensor("gdo", [128, 16], F32)
        gdb = Buf("gd")

        def allgather(gi, go, gib, gob):
            kb.emit("pool", lambda e: e.collective_compute("AllGather", ALU.bypass, replica_groups=[list(range(NCORE))],
                                                            ins=[gi.ap().opt()], outs=[go.ap().opt()]), [gib], [gob])
            if DOUBLE_CC:
                kb.emit("pool", lambda e: e.collective_compute("AllGather", ALU.bypass, replica_groups=[list(range(NCORE))],
                                                                ins=[gdi.ap().opt()], outs=[gdo.ap().opt()]), [gob, gib, gdb], [gob, gib, gdb])

        def setup():
            sc = contextlib.ExitStack()
            kb.dma("sp", cst[:], cst_d[:, :], [], [cstb], sem="cst")
            kb.emit("dve", lambda e: e.tensor_copy(out=ones_bf[:], in_=ones_f), [cstb], [cbf])
            kb.emit("dve", lambda e: e.tensor_copy(out=ident_bf[:], in_=ident), [cstb], [cbf])
            kb.emit("dve", lambda e: e.tensor_copy(out=le_bf[:], in_=cst[:, C_LE:C_LE + 128]), [cstb], [cbf])
            kb.emit("dve", lambda e: e.tensor_copy(out=gt_bf[:], in_=cst[:, C_GT:C_GT + 128]), [cstb], [cbf])
            posi = sb(sc, "posi", [128, T], I32)
            a = sb(sc, "ang", [128, T], F32)
            a2 = sb(sc, "ang2", [128, T], F32)
            kf = sb(sc, "kf", [128, T], F32)
            ki = sb(sc, "ki", [128, T], I32)
            tb_ = Buf("rope_tmp")
            kb.dma("sp", posi[:], pos_d[:, :], [], [tb_], sem="pos")
            kb.emit("dve", lambda e: e.tensor_copy(out=a[:], in_=posi[:]), [tb_], [tb_])
            kb.emit("dve", lambda e: e.tensor_scalar(out=a[:], in0=a[:], scalar1=cst[:, C_INVF:C_INVF + 1], scalar2=None, op0=ALU.mult), [tb_, cstb], [tb_])
            C1 = 6.28125
            C2 = 2.0 * math.pi - C1
            for which, shift, dstT in (("sin", 0.0, sinT), ("cos", math.pi / 2, cosT)):
                kb.emit("dve", lambda e: e.tensor_scalar(out=a2[:], in0=a[:], scalar1=shift, scalar2=None, op0=ALU.add), [tb_], [tb_])
                kb.emit("dve", lambda e: e.tensor_scalar(out=ki[:], in0=a2[:], scalar1=1.0 / (2 * math.pi), scalar2=None, op0=ALU.mult), [tb_], [tb_])
                kb.emit("dve", lambda e: e.tensor_copy(out=kf[:], in_=ki[:]), [tb_], [tb_])
                kb.emit("dve", lambda e: e.scalar_tensor_tensor(out=a2[:], in0=kf[:], scalar=-C1, in1=a2[:], op0=ALU.mult, op1=ALU.add), [tb_], [tb_])
                kb.emit("dve", lambda e: e.scalar_tensor_tensor(out=a2[:], in0=kf[:], scalar=-C2, in1=a2[:], op0=ALU.mult, op1=ALU.add), [tb_], [tb_])
                kb.emit("dve", lambda e: e.tensor_scalar(out=kf[:], in0=a2[:], scalar1=math.pi, scalar2=-2 * math.pi, op0=ALU.is_gt, op1=ALU.mult), [tb_], [tb_])
                kb.emit("dve", lambda e: e.tensor_tensor(out=a2[:], in0=a2[:], in1=kf[:], op=ALU.add), [tb_], [tb_])
                kb.emit("dve", lambda e: e.tensor_scalar(out=kf[:], in0=a2[:], scalar1=-math.pi, scalar2=2 * math.pi, op0=ALU.is_lt, op1=ALU.mult), [tb_], [tb_])
                kb.emit("dve", lambda e: e.tensor_tensor(out=a2[:], in0=a2[:], in1=kf[:], op=ALU.add), [tb_], [tb_])
                kb.emit("dve", lambda e: e.tensor_scalar(out=a2[:], in0=a2[:], scalar1=3.1415925, scalar2=-3.1415925, op0=ALU.min, op1=ALU.max), [tb_], [tb_])
                kb.emit("act", lambda e: e.activation(out=dstT[:], in_=a2[:], func=AF.Sin), [tb_], [tb_, csb])
            xs = [sb(sc, "xs%d" % i, [128, D], F32) for i in range(2)]
            xsb = [Buf("xs%d" % i) for i in range(2)]
            for tb in range(8):
                kb.dma("sp", xs[tb % 2][:], x_d[tb * 128:(tb + 1) * 128, :], [], [xsb[tb % 2]], sem="xs%d" % (tb % 2))
                for g4 in range(4):
                    ps, pb = kb.next_ps()
                    for i in range(4):
                        fc = g4 * 4 + i
                        kb.emit("pe", lambda e: e.transpose(ps[:, i * 128:(i + 1) * 128], xs[tb % 2][:, fc * 128:(fc + 1) * 128], ident), [xsb[tb % 2], cstb], [pb])
                    kb.emit("act", lambda e: e.activation(out=hT[:, g4 * 4:(g4 + 1) * 4, tb * 128:(tb + 1) * 128],
                                                           in_=ps[:].rearrange("p (c t) -> p c t", c=4), func=AF.Copy),
                            [pb], [hTb[g4 * 4 + i] for i in range(4)])
            kb.barrier()
            sc.close()

        add_call(setup)

        def layer_steps(l):
            def load_params():
                kb.dma("sp", pp[:], pp_d[l, :, :], [], [ppb], sem="pp")
                kb.dma("sp", wgf[:], wg_d[l, :, :], [], [wgbuf], sem="wg")
                kb.emit("dve", lambda e: e.tensor_copy(out=wgb[:], in_=wgf[:]), [wgbuf], [wgbuf])
            add_call(load_params)

            def mixer_open():
                o = contextlib.ExitStack()
                st["mx_o"] = o
                st["oa"] = sb(o, "oa", [128, 4, T], BF16)
                st["oab"] = [Buf("oa%d" % i) for i in range(4)]
                st["oc"] = sb(o, "oc", [128, 4, T], BF16)
                st["ocb"] = [Buf("oc%d" % i) for i in range(4)]
                st["xn"] = sb(o, "xn", [128, 16, T + 2], BF16)
                st["xnb"] = [Buf("xn%d" % i) for i in range(16)]
                sc = contextlib.ExitStack()
                rmsnorm(P_GMIX, st["xn"], st["xnb"], 2, sc)
                kb.barrier()
                sc.close()
                g = contextlib.ExitStack()
                st["gla"] = g
                st["qT"] = sb(g, "qT", [128, 2, 2, T], BF16)
                st["kT"] = sb(g, "kT", [128, 2, T], BF16)
                st["kh"] = sb(g, "kh", [128, 8, 256], BF16)
                st["vt"] = sb(g, "vt", [128, 8, 512], BF16)
                st["blast"] = sb(g, "blast", [128, 2, 8], F32)
                st["eblast"] = sb(g, "eblast", [128, 2, 8], F32)
                st["pay1"] = sb(g, "pay1", [128, 514], F32)
                g2 = contextlib.ExitStack()
                st["gla2"] = g2
                st["ebT"] = sb(g2, "ebT", [128, 2, T], BF16)
                st["enbT"] = sb(g2, "enbT", [128, 2, T], BF16)
                st["lrT"] = sb(g2, "lrT", [32, T], BF16)
                for n in ("qT", "kT", "kh", "vt", "ebT", "enbT", "lrT", "blast", "eblast", "pay1"):
                    st[n + "b"] = Buf(n)
                kb.emit("dve", lambda e: e.memset(st["lrT"][:], 1.0), [], [st["lrTb"]])
                kb.emit("dve", lambda e: e.memset(st["qT"][:], 0.0), [], [st["qTb"]])
            add_call(mixer_open)

            def xn_rhs(k, tt):
                return st["xn"][:, k, 2 + tt * 512:2 + (tt + 1) * 512], st["xnb"][k]

            def xn_lhs(k, tb):
                return st["xn"][:, k, 2 + tb * 128:2 + (tb + 1) * 128], st["xnb"][k]

            def h_glr(slot, slotb):
                def ep(ci, tt, ps, pb):
                    act_copy(st["lrT"][0:16, tt * 512:(tt + 1) * 512], ps[0:16, :], [pb], [st["lrTb"]])
                proj_fm(slot, slotb, [(0, 16)], xn_rhs, 16, ep)
                import os
                SUB = int(os.environ.get("SUBCUT", "9"))
                if SUB <= 1:
                    return
                g = st["gla2"]
                ex = sb(g, "ex", [128, 256], F32)
                sp = sb(g, "sp", [128, 256], F32)
                exb, spb = Buf("ex"), Buf("sp")
                for tb in range(8):
                    ps, pb = kb.next_ps()
                    kb.mm(ps[:, 0:256], st["lrT"][0:17, tb * 128:(tb + 1) * 128], wgb[0:17, :], True, True, [st["lrTb"], wgbuf], [pb])
                    if SUB <= 2:
                        continue
                    kb.emit("act", lambda e: e.activation(out=ex[:], in_=ps[:, 0:256], func=AF.Exp, scale=-1.0), [pb], [exb])
                    kb.emit("act", lambda e: e.activation(out=sp[:], in_=ex[:], func=AF.Ln, bias=1.0), [exb], [spb])
                    if SUB <= 3:
                        continue
                    for j in range(2):
                        ps2, pb2 = kb.next_ps()
                        kb.mm(ps2[:, 0:128], sp[:, j * 128:(j + 1) * 128], tri, True, True, [spb, cstb], [pb2])
                        if SUB <= 4:
                            continue
                        cs = slice(tb * 128, (tb + 1) * 128)
                        kb.emit("act", lambda e: e.activation(out=st["ebT"][:, j, cs], in_=ps2[:, 0:128], func=AF.Exp), [pb2], [st["ebTb"]])
                        kb.emit("act", lambda e: e.activation(out=st["enbT"][:, j, cs], in_=ps2[:, 0:128], func=AF.Exp, scale=-1.0), [pb2], [st["enbTb"]])
                        kb.emit("dve", lambda e: e.tensor_copy(out=st["blast"][:, j, tb:tb + 1], in_=ps2[:, 127:128]), [pb2], [st["blastb"]])
                if SUB <= 5:
                    return
                kb.emit("act", lambda e: e.activation(out=st["eblast"][:], in_=st["blast"][:], func=AF.Exp), [st["blastb"]], [st["eblastb"]])
                for j in range(2):
                    kb.emit("dve", lambda e: e.tensor_reduce(out=st["pay1"][:, 512 + j:513 + j], in_=st["blast"][:, j, :], axis=mybir.AxisListType.X, op=ALU.add),
                            [st["blastb"]], [st["pay1b"]])
            add_job([(0, 16, 0, 16, wsrc(w_in, l, 0, 16, O_LR, 16))], h_glr)

            def h_gq(slot, slotb):
                def ep(ci, tt, ps, pb):
                    ts = slice(tt * 512, (tt + 1) * 512)
                    for hp in range(2):
                        hs = slice(hp * 64, (hp + 1) * 64)
                        kb.emit("dve", lambda e: e.scalar_tensor_tensor(out=st["qT"][hs, hp, ci, ts], in0=ps[hs, :], scalar=0.125, in1=st["ebT"][hs, ci, ts], op0=ALU.mult, op1=ALU.mult),
                                [pb, st["ebTb"]], [st["qTb"]])
                proj_fm(slot, slotb, [(0, 128), (128, 128)], xn_rhs, 16, ep)
            add_job([(0, 16, 0, 256, wsrc(w_in, l, 0, 16, O_GQ, 256))], h_gq)

            def h_gk(slot, slotb):
                def ep(ci, tt, ps, pb):
                    ts = slice(tt * 512, (tt + 1) * 512)
                    kb.emit("dve", lambda e: e.tensor_tensor(out=st["kT"][:, ci, ts], in0=ps[:], in1=st["enbT"][:, ci, ts], op=ALU.mult), [pb, st["enbTb"]], [st["kTb"]])
                proj_fm(slot, slotb, [(0, 128), (128, 128)], xn_rhs, 16, ep)
                g = st["gla2"]
                ktmp = [sb(g, "ktmp%d" % i, [128, 128], BF16) for i in range(2)]
                ktb = [Buf("ktmp%d" % i) for i in range(2)]
                n = 0
                for j in range(2):
                    for tb in range(8):
                        kt, ktbb = ktmp[n % 2], ktb[n % 2]
                        n += 1
                        kb.emit("dve", lambda e: e.tensor_scalar(out=kt[:], in0=st["kT"][:, j, tb * 128:(tb + 1) * 128], scalar1=st["eblast"][:, j, tb:tb + 1], scalar2=None, op0=ALU.mult),
                                [st["kTb"], st["eblastb"]], [ktbb])
                        ps, pb = kb.next_ps()
                        pst = ps[:, 0:64].bitcast(BF16)
                        kb.emit("pe", lambda e: e.transpose(pst, kt[:], ident_bf[:]), [ktbb, cbf], [pb])
                        act_copy(st["kh"][:, tb, j * 128:(j + 1) * 128], pst, [pb], [st["khb"]])
            add_job([(0, 16, 0, 256, wsrc(w_in, l, 0, 16, O_GK, 256))], h_gk)

            for half in range(2):
                def h_gv(slot, slotb, half=half):
                    def ep(tb, ps, pb):
                        act_copy(st["vt"][:, tb, half * 256:(half + 1) * 256], ps[:, 0:256], [pb], [st["vtb"]])
                    proj_tm(slot, slotb, 0, 256, xn_lhs, 16, ep)
                add_job([(0, 16, 0, 256, wsrc(w_in, l, 0, 16, O_GV + half * 256, 256))], h_gv)

            for half in range(2):
                def h_gr(slot, slotb, half=half):
                    def ep(ci, tt, ps, pb):
                        kb.emit("act", lambda e: e.activation(out=st["oa"][:, half * 2 + ci, tt * 512:(tt + 1) * 512], in_=ps[:], func=AF.Silu), [pb], [st["oab"][half * 2 + ci]])
                    proj_fm(slot, slotb, [(0, 128), (128, 128)], xn_rhs, 16, ep)
                add_job([(0, 16, 0, 256, wsrc(w_in, l, 0, 16, O_GR + half * 256, 256))], h_gr)

            def gla_core():
                kb.barrier()
                st["gla2"].close()
                g = st["gla"]
                S = st["pay1"][:, 0:512].rearrange("p (j e) -> p j e", j=2)
                Sb = st["pay1b"]
                kb.emit("dve", lambda e: e.memset(st["pay1"][:, 0:512], 0.0), [], [Sb])

                def kv_mm(tb, j):
                    ps, pb = kb.next_ps()
                    kb.mm(ps[:, 0:256], st["kh"][:, tb, j * 128:(j + 1) * 128], st["vt"][:, tb, j * 256:(j + 1) * 256], True, True, [st["khb"], st["vtb"]], [pb])
                    return ps, pb
                for tb in range(8):
                    for j in range(2):
                        ps, pb = kv_mm(tb, j)
                        kb.emit("dve", lambda e: e.scalar_tensor_tensor(out=S[:, j, :], in0=S[:, j, :], scalar=st["eblast"][:, j, tb:tb + 1], in1=ps[:, 0:256], op0=ALU.mult, op1=ALU.add),
                                [pb, st["eblastb"]], [Sb])
                import os
                SUB = int(os.environ.get("SUBG", "9"))
                if SUB <= 1:
                    return
                kb.dma("sp", gin1.ap()[:, :], st["pay1"][:], [Sb], [gb["gin1"]], sem="g1")
                allgather(gin1, gout1, gb["gin1"], gb["gout1"])
                if SUB <= 2:
                    kb.emit("dve", lambda e: e.memset(st["pay1"][:, 0:2], 0.0), [gb["gout1"]], [Sb])
                    return
                lt = cst[:, C_LT:C_LT + 8]
                suf = sb(g, "suf", [128, 2], F32)
                Mr = sb(g, "Mr", [128, 2], F32)
                Mb = Buf("Mw")
                kb.emit("dve", lambda e: e.memset(suf[:], 0.0), [], [Mb])
                Sin = sb(g, "Sin", [128, 2, 256], F32)
                Sinb = Buf("Sin")
                kb.emit("dve", lambda e: e.memset(Sin[:], 0.0), [], [Sinb])
                Sall = sb(g, "Sall", [128, 8, 514], F32)
                srb = Buf("Sall")
                kb.dma("sp", Sall[:], gout1.ap().rearrange("(r p) n -> p r n", p=128), [gb["gout1"]], [srb], sem="sst")
                for r in range(7, -1, -1):
                    sr = Sall[:, r, :]
                    VV = os.environ.get("VV", "")
                    if "a" not in VV:
                        kb.emit("act", lambda e: e.activation(out=Mr[:], in_=suf[:], func=AF.Exp), [Mb], [Mb])
                    else:
                        kb.emit("dve", lambda e: e.memset(Mr[:], 1.0), [], [Mb])
                    kb.emit("dve", lambda e: e.tensor_scalar(out=Mr[:], in0=Mr[:], scalar1=lt[:, r:r + 1], scalar2=None, op0=ALU.mult), [Mb, cstb], [Mb])
                    if "b" not in VV:
                        for j in range(2):
                            kb.emit("dve", lambda e: e.scalar_tensor_tensor(out=Sin[:, j, :], in0=sr[:, j * 256:(j + 1) * 256], scalar=Mr[:, j:j + 1], in1=Sin[:, j, :], op0=ALU.mult, op1=ALU.add),
                                    [srb, Mb], [Sinb])
                    if "c" not in VV:
                        kb.emit("dve", lambda e: e.scalar_tensor_tensor(out=suf[:], in0=sr[:, 512:514], scalar=lt[:, r:r + 1], in1=suf[:], op0=ALU.mult, op1=ALU.add), [srb, Mb, cstb], [Mb])
                if SUB <= 3:
                    return
                Sbf = sb(g, "Sbf", [128, 2, 256], BF16)
                Sbfb = Buf("Sbf")
                attm = sb(g, "attm", [128, 4, 128], BF16)
                attmb = Buf("attm")
                sqo = sb(g, "sqo", [128, 512], BF16)
                sqob = Buf("sqo")
                rso = sb(g, "rso", [128, 512], F32)
                rsob = Buf("rso")
                on = sb(g, "on", [128, 512], F32)
                onb = Buf("on")
                for tb in range(8):
                    cs = slice(tb * 128, (tb + 1) * 128)
                    act_copy(Sbf[:], Sin[:], [Sinb], [Sbfb])
                    psa, pba = kb.next_ps()
                    for h in range(4):
                        j, hp = h // 2, h % 2
                        kb.mm(psa[:, h * 128:(h + 1) * 128], st["kT"][:, j, cs], st["qT"][:, hp, j, cs], True, True, [st["kTb"], st["qTb"]], [pba])
                    kb.emit("dve", lambda e: e.tensor_tensor(out=attm[:], in0=psa[:].rearrange("p (h t) -> p h t", h=4), in1=le_bf[:].unsqueeze(1).broadcast_to([128, 4, 128]), op=ALU.mult),
                            [pba, cbf], [attmb])
                    pso, pbo = kb.next_ps()
                    for h in range(4):
                        j, hp = h // 2, h % 2
                        kb.mm(pso[:, h * 128:(h + 1) * 128], st["vt"][:, tb, h * 128:(h + 1) * 128], attm[:, h, :], True, False, [st["vtb"], attmb], [pbo], inc=False)
                        kb.mm(pso[:, h * 128:(h + 1) * 128], Sbf[:, j, hp * 128:(hp + 1) * 128], st["qT"][:, hp, j, cs], False, True, [Sbfb, st["qTb"]], [pbo])
                    kb.emit("act", lambda e: e.activation(out=sqo[:], in_=pso[:], func=AF.Square), [pbo], [sqob])
                    psn, pbn = kb.next_ps()
                    kb.mm(psn[:], ones_bf[:], sqo[:], True, True, [sqob, cbf], [pbn])
                    kb.emit("dve", lambda e: e.tensor_scalar(out=rso[:], in0=psn[:], scalar1=1.0 / 128, scalar2=EPS, op0=ALU.mult, op1=ALU.add), [pbn], [rsob])
                    kb.emit("act", lambda e: e.activation(out=rso[:], in_=rso[:], func=AF.Sqrt), [rsob], [rsob])
                    kb.emit("dve", lambda e: e.reciprocal(out=rso[:], in_=rso[:]), [rsob], [rsob])
                    kb.emit("dve", lambda e: e.tensor_tensor(out=on[:], in0=pso[:], in1=rso[:], op=ALU.mult), [pbo, rsob], [onb])
                    kb.emit("dve", lambda e: e.scalar_tensor_tensor(out=st["oa"][:, :, cs], in0=on[:].rearrange("p (h t) -> p h t", h=4), scalar=pp[:, P_GLAN:P_GLAN + 1], in1=st["oa"][:, :, cs], op0=ALU.mult, op1=ALU.mult),
                            [onb, ppb] + st["oab"], st["oab"])
                    if tb < 7:
                        for j in range(2):
                            ps, pb = kv_mm(tb, j)
                            kb.emit("dve", lambda e: e.scalar_tensor_tensor(out=Sin[:, j, :], in0=Sin[:, j, :], scalar=st["eblast"][:, j, tb:tb + 1], in1=ps[:, 0:256], op0=ALU.mult, op1=ALU.add),
                                    [pb, st["eblastb"]], [Sinb])
                kb.barrier()
                st["gla"].close()
                s = contextlib.ExitStack()
                st["swa"] = s
                st["Qr"] = sb(s, "Qr", [128, 8, T], BF16)
                st["Qrb"] = [Buf("Qr%d" % i) for i in range(8)]
                st["Kr"] = sb(s, "Kr", [128, 128 + T], BF16)
                st["Vt"] = sb(s, "Vt", [128, 9, 128], BF16)
                st["u"] = sb(s, "u", [128, 4, T + 2], BF16)
                st["cbf2"] = sb(s, "cbf2", [128, 4, 2], F32)
                st["pay2"] = sb(s, "pay2", [128, 264], F32)
                st["qf"] = sb(s, "qf", [128, 512], F32)
                st["t1"] = sb(s, "t1", [128, 512], F32)
                st["t2"] = sb(s, "t2", [128, 512], F32)
                for n in ("Kr", "Vt", "u", "cbf2", "pay2", "qf", "t1", "t2"):
                    st[n + "b"] = Buf(n)
                kb.emit("dve", lambda e: e.memset(st["u"][:, :, 0:2], 0.0), [], [st["ub"]])
            add_call(gla_core)

            def rope_ep(ps, pb, tt, outs):
                ts = slice(tt * 512, (tt + 1) * 512)
                act_copy(st["qf"][:], ps[:], [pb], [st["qfb"]])
                ps2, pb2 = kb.next_ps()
                kb.mm(ps2[:], rotP, st["qf"][:], True, True, [st["qfb"], cstb], [pb2])
                kb.emit("dve", lambda e: e.tensor_tensor(out=st["t1"][:], in0=st["qf"][:], in1=cosT[:, ts], op=ALU.mult), [st["qfb"], csb], [st["t1b"]])
                kb.emit("dve", lambda e: e.tensor_tensor(out=st["t2"][:], in0=ps2[:], in1=sinT[:, ts], op=ALU.mult), [pb2, csb], [st["t2b"]])
                for (oap, c0, c1, bufs) in outs:
                    kb.emit("dve", lambda e: e.tensor_tensor(out=oap, in0=st["t1"][:, c0:c1], in1=st["t2"][:, c0:c1], op=ALU.add), [st["t1b"], st["t2b"]], bufs)

            for m in range(4):
                def h_sq(slot, slotb, m=m):
                    def ep(ci, tt, ps, pb):
                        ch = 2 * m + ci
                        rope_ep(ps, pb, tt, [(st["Qr"][:, ch, tt * 512:(tt + 1) * 512], 0, 512, [st["Qrb"][ch]])])
                    proj_fm(slot, slotb, [(0, 128), (128, 128)], xn_rhs, 16, ep)
                parts = []
                for ci in range(2):
                    i = 2 * m + ci
                    parts.append((0, 16, ci * 128, 64, wsrc(w_in, l, 0, 16, O_SQ + 64 * i, 64)))
                    parts.append((0, 16, ci * 128 + 64, 64, wsrc(w_in, l, 0, 16, O_SQ + 64 * (8 + i), 64)))
                add_job(parts, h_sq)

            def h_skv(slot, slotb):
                def ep(ci, tt, ps, pb):
                    outs = [(st["Kr"][:, 128 + tt * 512:128 + (tt + 1) * 512], 0, 512, [st["Krb"]])]
                    if tt == 1:
                        outs.append((st["pay2"][:, 0:128], 384, 512, [st["pay2b"]]))
                    rope_ep(ps, pb, tt, outs)
                proj_fm(slot, slotb, [(0, 128)], xn_rhs, 16, ep)

                def epv(tb, ps, pb):
                    act_copy(st["Vt"][:, 1 + tb, :], ps[:, 0:128], [pb], [st["Vtb"]])
                    if tb == 7:
                        act_copy(st["pay2"][:, 128:256], ps[:, 0:128], [pb], [st["pay2b"]])
                proj_tm(slot, slotb, 128, 128, xn_lhs, 16, epv)
            add_job([(0, 16, 0, 256, wsrc(w_in, l, 0, 16, O_SK, 256))], h_skv)

            for half in range(2):
                def h_ch(slot, slotb, half=half):
                    def ep(ci, tt, ps, pb):
                        act_copy(st["u"][:, half * 2 + ci, 2 + tt * 512:2 + (tt + 1) * 512], ps[:], [pb], [st["ub"]])
                    proj_fm(slot, slotb, [(0, 128), (128, 128)], xn_rhs, 16, ep)
                add_job([(0, 16, 0, 256, wsrc(w_in, l, 0, 16, O_CH + half * 256, 256))], h_ch)
            for half in range(2):
                def h_cc(slot, slotb, half=half):
                    def ep(ci, tt, ps, pb):
                        ua = st["u"][:, half * 2 + ci, 2 + tt * 512:2 + (tt + 1) * 512]
                        kb.emit("dve", lambda e: e.tensor_tensor(out=ua, in0=ps[:], in1=ua, op=ALU.mult), [pb, st["ub"]], [st["ub"]])
                        if tt == 1:
                            j = half * 2 + ci
                            kb.emit("dve", lambda e: e.tensor_copy(out=st["pay2"][:, 256 + 2 * j:258 + 2 * j], in_=st["u"][:, j, T:T + 2]), [st["ub"]], [st["pay2b"]])
                    proj_fm(slot, slotb, [(0, 128), (128, 128)], xn_rhs, 16, ep)
                add_job([(0, 16, 0, 256, wsrc(w_in, l, 0, 16, O_CC + half * 256, 256))], h_cc)

            def conv_oc(j, cb_ap, cb_bufs, t0, n):
                cv = st["t1"][:, 0:n]
                w = lambda tap: pp[:, P_SCONV + j * 3 + tap:P_SCONV + j * 3 + tap + 1]
                kb.emit("dve", lambda e: e.tensor_scalar(out=cv, in0=st["u"][:, j, t0:t0 + n], scalar1=w(0), scalar2=None, op0=ALU.mult), [st["ub"], ppb], [st["t1b"]])
                kb.emit("dve", lambda e: e.scalar_tensor_tensor(out=cv, in0=st["u"][:, j, t0 + 1:t0 + 1 + n], scalar=w(1), in1=cv, op0=ALU.mult, op1=ALU.add), [st["ub"], ppb], [st["t1b"]])
                kb.emit("dve", lambda e: e.scalar_tensor_tensor(out=cv, in0=st["u"][:, j, t0 + 2:t0 + 2 + n], scalar=w(2), in1=cv, op0=ALU.mult, op1=ALU.add), [st["ub"], ppb], [st["t1b"]])
                kb.emit("dve", lambda e: e.tensor_tensor(out=st["oc"][:, j, t0:t0 + n], in0=cb_ap, in1=cv, op=ALU.mult), cb_bufs + [st["t1b"]], [st["ocb"][j]])

            for half in range(2):
                def h_cb(slot, slotb, half=half):
                    def ep(ci, tt, ps, pb):
                        j = half * 2 + ci
                        if tt == 0:
                            kb.emit("dve", lambda e: e.tensor_copy(out=st["cbf2"][:, j, :], in_=ps[:, 0:2]), [pb], [st["cbf2b"]])
                        conv_oc(j, ps[:], [pb], tt * 512, 512)
                    proj_fm(slot, slotb, [(0, 128), (128, 128)], xn_rhs, 16, ep)
                add_job([(0, 16, 0, 256, wsrc(w_in, l, 0, 16, O_CB + half * 256, 256))], h_cb)

            def swa_core():
                s = st["swa"]
                kb.dma("sp", gin2.ap()[:, :], st["pay2"][:], [st["pay2b"]], [gb["gin2"]], sem="g2")
                allgather(gin2, gout2, gb["gin2"], gb["gout2"])
                hal = sb(s, "hal", [128, 8, 264], F32)
                halb = Buf("hal")
                kb.dma("sp", hal[:], gout2.ap().rearrange("(r p) n -> p r n", p=128), [gb["gout2"]], [halb], sem="g2l")
                acc = sb(s, "hacc", [128, 264], F32)
                accb = Buf("hacc")
                kb.emit("dve", lambda e: e.tensor_scalar(out=acc[:], in0=hal[:, 0, :], scalar1=cst[:, C_SEL:C_SEL + 1], scalar2=None, op0=ALU.mult), [halb, cstb], [accb])
                for r in range(1, 8):
                    kb.emit("dve", lambda e: e.scalar_tensor_tensor(out=acc[:], in0=hal[:, r, :], scalar=cst[:, C_SEL + r:C_SEL + r + 1], in1=acc[:], op0=ALU.mult, op1=ALU.add), [halb, cstb], [accb])
                esk = sb(s, "esk", [128, 8], F32)
                esb = Buf("esk")
                kb.emit("act", lambda e: e.activation(out=esk[:], in_=pp[:, P_SINK:P_SINK + 8], func=AF.Exp), [ppb], [esb])
                PT = [sb(s, "PT%d" % i, [128, 512], BF16) for i in range(4)]
                PTb = [Buf("PT%d" % i) for i in range(4)]
                rd = sb(s, "rd", [128, 512], F32)
                rdb = Buf("rd")
                vo = sb(s, "vones", [128, 64], BF16)
                kb.emit("dve", lambda e: e.memset(vo[:], 1.0), [], [cbf])

                def block(n):
                    for a in range(2):
                        pso, pbo = kb.next_ps()
                        psd, pbd = kb.next_ps()
                        for g in range(2):
                            gp = slice(g * 64, (g + 1) * 64)
                            for kbi in range(2):
                                kc = slice((n + kbi) * 128, (n + kbi + 1) * 128)
                                pss, pbs = kb.next_ps()
                                kb.mm(pss[:], st["Kr"][gp, kc], st["Qr"][gp, 4 * a:4 * a + 4, n * 128:(n + 1) * 128], True, True,
                                      [st["Krb"]] + st["Qrb"][4 * a:4 * a + 4], [pbs])
                                P = PT[g * 2 + kbi]
                                Pb = PTb[g * 2 + kbi]
                                kb.emit("act", lambda e: e.activation(out=P[:], in_=pss[:], func=AF.Exp, scale=0.125), [pbs], [Pb])
                                msk = gt_bf if kbi == 0 else le_bf
                                kb.emit("dve", lambda e: e.tensor_tensor(out=P[:].rearrange("p (h t) -> p h t", h=4), in0=P[:].rearrange("p (h t) -> p h t", h=4),
                                                                         in1=msk[:].unsqueeze(1).broadcast_to([128, 4, 128]), op=ALU.mult), [Pb, cbf], [Pb])
                                if kbi == 0 and n == 0:
                                    kb.emit("dve", lambda e: e.tensor_scalar(out=P[:], in0=P[:], scalar1=cst[:, C_HASPREV:C_HASPREV + 1], scalar2=None, op0=ALU.mult), [Pb, cstb], [Pb])
                            for kbi in range(2):
                                P = PT[g * 2 + kbi]
                                Pb = PTb[g * 2 + kbi]
                                kb.mm(pso[gp, :], st["Vt"][:, n + kbi, gp], P[:], kbi == 0, kbi == 1, [st["Vtb"], Pb], [pbo])
                            for kbi in range(2):
                                P = PT[g * 2 + kbi]
                                Pb = PTb[g * 2 + kbi]
                                kb.mm(psd[gp, :], vo[:], P[:], kbi == 0, kbi == 1, [cbf, Pb], [pbd])
                        kb.emit("dve", lambda e: e.tensor_tensor(out=rd[:].rearrange("p (h t) -> p h t", h=4), in0=psd[:].rearrange("p (h t) -> p h t", h=4),
                                                                 in1=esk[:, 4 * a:4 * a + 4].unsqueeze(2).broadcast_to([128, 4, 128]), op=ALU.add), [pbd, esb], [rdb])
                        kb.emit("dve", lambda e: e.reciprocal(out=rd[:], in_=rd[:]), [rdb], [rdb])
                        kb.emit("dve", lambda e: e.tensor_tensor(out=st["Qr"][:, 4 * a:4 * a + 4, n * 128:(n + 1) * 128], in0=pso[:].rearrange("p (h t) -> p h t", h=4),
                                                                 in1=rd[:].rearrange("p (h t) -> p h t", h=4), op=ALU.mult),
                                [pbo, rdb], st["Qrb"][4 * a:4 * a + 4])
                for n in range(1, 8):
                    block(n)
                kb.emit("dve", lambda e: e.tensor_copy(out=st["Kr"][:, 0:128], in_=acc[:, 0:128]), [accb], [st["Krb"]])
                kb.emit("dve", lambda e: e.tensor_copy(out=st["Vt"][:, 0, :], in_=acc[:, 128:256]), [accb], [st["Vtb"]])
                kb.emit("dve", lambda e: e.tensor_copy(out=st["u"][:, :, 0:2], in_=acc[:, 256:264].rearrange("p (j t) -> p j t", j=4)), [accb], [st["ub"]])
                block(0)
                for j in range(4):
                    conv_oc(j, st["cbf2"][:, j, :], [st["cbf2b"]], 0, 2)
            add_call(swa_core)

            def mix_rhs(k, tt):
                ts = slice(tt * 512, (tt + 1) * 512)
                if k < 4:
                    return st["oa"][:, k, ts], st["oab"][k]
                if k < 12:
                    return st["Qr"][:, k - 4, ts], st["Qrb"][k - 4]
                return st["oc"][:, k - 12, ts], st["ocb"][k - 12]

            def resid_ep(n0):
                def ep(ci, tt, ps, pb):
                    ch = n0 + ci
                    ts = slice(tt * 512, (tt + 1) * 512)
                    kb.emit("dve", lambda e: e.tensor_tensor(out=hT[:, ch, ts], in0=ps[:], in1=hT[:, ch, ts], op=ALU.add), [pb, hTb[ch]], [hTb[ch]])
                return ep

            for nn in range(8):
                def h_wo(slot, slotb, nn=nn):
                    proj_fm(slot, slotb, [(0, 128), (128, 128)], mix_rhs, 16, resid_ep(nn * 2))
                parts = [(0, 4, 0, 256, wsrc(w_out, l, 0, 4, nn * 256, 256))]
                for i in range(8):
                    for g in range(2):
                        r0 = 512 + (g * 8 + i) * 64
                        parts.append((4 + i, 1, 0, 256, w_out[l, r0:r0 + 64, nn * 256:nn * 256 + 256], g * 64, 64))
                parts.append((12, 4, 0, 256, wsrc(w_out, l, 12, 4, nn * 256, 256)))
                add_job(parts, h_wo)

            def mixer_close():
                kb.barrier()
                st["swa"].close()
                st["mx_o"].close()
            add_call(mixer_close)
            if stop_after == "mixer" and l == NL - 1:
                return

            def xa_open():
                o = contextlib.ExitStack()
                st["xa_o"] = o
                st["xn"] = sb(o, "xn", [128, 16, T + 2], BF16)
                st["xnb"] = [Buf("xn%d" % i) for i in range(16)]
                st["memnT"] = sb(o, "memnT", [128, 16, NMEM], BF16)
                st["memnTb"] = Buf("memnT")
                sc = contextlib.ExitStack()
                rmsnorm(P_GX, st["xn"], st["xnb"], 2, sc)
                ms = [sb(sc, "ms%d" % i, [128, D], F32) for i in range(2)]
                msb = [Buf("ms%d" % i) for i in range(2)]
                junk = sb(sc, "junk", [128, D], BF16)
                ss = sb(sc, "mss", [128, 2], F32)
                jb = Buf("junk")
                for mb in range(2):
                    kb.dma("sp", ms[mb][:], mem_d[mb * 128:(mb + 1) * 128, :], [], [msb[mb]], sem="ms%d" % mb)
                    kb.emit("act", lambda e: e.activation(out=junk[:], in_=ms[mb][:], func=AF.Square, accum_out=ss[:, mb:mb + 1]), [msb[mb]], [jb])
                    kb.emit("dve", lambda e: e.tensor_scalar(out=ss[:, mb:mb + 1], in0=ss[:, mb:mb + 1], scalar1=1.0 / D, scalar2=EPS, op0=ALU.mult, op1=ALU.add), [jb], [jb])
                    kb.emit("act", lambda e: e.activation(out=ss[:, mb:mb + 1], in_=ss[:, mb:mb + 1], func=AF.Sqrt), [jb], [jb])
                    kb.emit("dve", lambda e: e.reciprocal(out=ss[:, mb:mb + 1], in_=ss[:, mb:mb + 1]), [jb], [jb])
                    kb.emit("dve", lambda e: e.tensor_scalar(out=ms[mb][:], in0=ms[mb][:], scalar1=ss[:, mb:mb + 1], scalar2=None, op0=ALU.mult), [jb, msb[mb]], [msb[mb]])
                    for g4 in range(4):
                        ps, pb = kb.next_ps()
                        for i in range(4):
                            fc = g4 * 4 + i
                            kb.emit("pe", lambda e: e.transpose(ps[:, i * 128:(i + 1) * 128], ms[mb][:, fc * 128:(fc + 1) * 128], ident), [msb[mb], cstb], [pb])
                        for i in range(4):
                            fc = g4 * 4 + i
                            kb.emit("dve", lambda e: e.tensor_scalar(out=st["memnT"][:, fc, mb * 128:(mb + 1) * 128], in0=ps[:, i * 128:(i + 1) * 128],
                                                                      scalar1=pp[:, P_GMEM + fc:P_GMEM + fc + 1], scalar2=None, op0=ALU.mult), [pb, ppb], [st["memnTb"]])
                kb.barrier()
                sc.close()
                o = contextlib.ExitStack()
                st["xa_i"] = o
                st["oT"] = sb(o, "oT", [128, 16, T], BF16)
                st["oTb"] = [Buf("oT%d" % i) for i in range(16)]
                st["KT"] = sb(o, "KT", [128, 16, NMEM], BF16)
                st["Vx"] = sb(o, "Vx", [128, 2, D], BF16)
                st["qh"] = sb(o, "qh", [128, 4, T], BF16)
                st["PX"] = [sb(o, "PX%d" % i, [128, 512], BF16) for i in range(2)]
                st["PXb"] = [Buf("PX%d" % i) for i in range(2)]
                st["rdx"] = sb(o, "rdx", [128, 512], F32)
                for n in ("KT", "Vx", "qh", "rdx"):
                    st[n + "b"] = Buf(n)
            add_call(xa_open)

            def mem_rhs(k, tt):
                return st["memnT"][:, k, :], st["memnTb"]

            def mem_lhs(k, mb):
                return st["memnT"][:, k, mb * 128:(mb + 1) * 128], st["memnTb"]

            for nn in range(8):
                def h_wk(slot, slotb, nn=nn):
                    def ep(ci, tt, ps, pb):
                        act_copy(st["KT"][:, nn * 2 + ci, :], ps[:, 0:NMEM], [pb], [st["KTb"]])
                    proj_fm(slot, slotb, [(0, 128), (128, 128)], mem_rhs, 16, ep, ntt=1)
                add_job([(0, 16, 0, 256, wsrc(wk_d, l, 0, 16, nn * 256, 256))], h_wk)
            for nn in range(8):
                def h_wv(slot, slotb, nn=nn):
                    def ep(mb, ps, pb):
                        act_copy(st["Vx"][:, mb, nn * 256:(nn + 1) * 256], ps[:, 0:256], [pb], [st["Vxb"]])
                    proj_tm(slot, slotb, 0, 256, mem_lhs, 16, ep, ntb=2)
                add_job([(0, 16, 0, 256, wsrc(wv_d, l, 0, 16, nn * 256, 256))], h_wv)

            XSC = 512.0 ** -0.5
            for h in range(4):
                for half in range(2):
                    def h_wq(slot, slotb, h=h, half=half):
                        def ep(ci, tt, ps, pb):
                            act_copy(st["qh"][:, half * 2 + ci, tt * 512:(tt + 1) * 512], ps[:], [pb], [st["qhb"]])
                        proj_fm(slot, slotb, [(0, 128), (128, 128)], xn_rhs, 16, ep)
                        if half == 0:
                            return
                        for tt in range(2):
                            ts = slice(tt * 512, (tt + 1) * 512)
                            for mb in range(2):
                                pss, pbs = kb.next_ps()
                                for c in range(4):
                                    kb.mm(pss[:], st["KT"][:, 4 * h + c, mb * 128:(mb + 1) * 128], st["qh"][:, c, ts], c == 0, c == 3, [st["KTb"], st["qhb"]], [pbs], inc=(c == 3))
                                kb.emit("act", lambda e: e.activation(out=st["PX"][mb][:], in_=pss[:], func=AF.Exp, scale=XSC), [pbs], [st["PXb"][mb]])
                            psd, pbd = kb.next_ps()
                            for mb in range(2):
                                kb.mm(psd[:], ones_bf[:], st["PX"][mb][:], mb == 0, mb == 1, [cbf, st["PXb"][mb]], [pbd], inc=(mb == 1))
                            kb.emit("dve", lambda e: e.reciprocal(out=st["rdx"][:], in_=psd[:]), [pbd], [st["rdxb"]])
                            for c in range(4):
                                pso, pbo = kb.next_ps()
                                for mb in range(2):
                                    kb.mm(pso[:], st["Vx"][:, mb, (4 * h + c) * 128:(4 * h + c + 1) * 128], st["PX"][mb][:], mb == 0, mb == 1, [st["Vxb"], st["PXb"][mb]], [pbo], inc=(mb == 1))
                                kb.emit("dve", lambda e: e.tensor_tensor(out=st["oT"][:, 4 * h + c, ts], in0=pso[:], in1=st["rdx"][:], op=ALU.mult), [pbo, st["rdxb"]], [st["oTb"][4 * h + c]])
                    add_job([(0, 16, 0, 256, wsrc(wq_d, l, 0, 16, h * 512 + half * 256, 256))], h_wq)

            def oT_rhs(k, tt):
                return st["oT"][:, k, tt * 512:(tt + 1) * 512], st["oTb"][k]
            for nn in range(8):
                def h_xo(slot, slotb, nn=nn):
                    proj_fm(slot, slotb, [(0, 128), (128, 128)], oT_rhs, 16, resid_ep(nn * 2))
                add_job([(0, 16, 0, 256, wsrc(wo_d, l, 0, 16, nn * 256, 256))], h_xo)

            def xa_close():
                kb.barrier()
                st["xa_i"].close()
                st["xa_o"].close()
            add_call(xa_close)
            if stop_after == "xa" and l == NL - 1:
                return

            def ffn_open():
                o = contextlib.ExitStack()
                st["ffn_o"] = o
                st["xn"] = sb(o, "xn", [128, 16, T + 2], BF16)
                st["xnb"] = [Buf("xn%d" % i) for i in range(16)]
                st["act"] = sb(o, "actb", [128, 11, T], BF16)
                st["actb"] = [Buf("act%d" % i) for i in range(11)]
                st["ug"] = [sb(o, "ug%d" % i, [128, T + 2], F32) for i in range(2)]
                st["uv"] = [sb(o, "uv%d" % i, [128, T + 2], F32) for i in range(2)]
                st["ugb"] = [Buf("ug%d" % i) for i in range(2)]
                st["uvb"] = [Buf("uv%d" % i) for i in range(2)]
                st["cg"] = sb(o, "cg", [128, 512], F32)
                st["cv"] = sb(o, "cv", [128, 512], F32)
                st["sg"] = sb(o, "sg", [128, 512], F32)
                for n in ("cg", "cv", "sg"):
                    st[n + "b"] = Buf(n)
                sc = contextlib.ExitStack()
                rmsnorm(P_GFFN, st["xn"], st["xnb"], 2, sc)
                pay3 = sb(sc, "pay3", [128, 32], F32)
                p3b = Buf("pay3")
                kb.emit("dve", lambda e: e.tensor_copy(out=pay3[:].rearrange("p (k t) -> p k t", k=16), in_=st["xn"][:, :, T:T + 2]), st["xnb"], [p3b])
                kb.dma("sp", gin3.ap()[:, :], pay3[:], [p3b], [gb["gin3"]], sem="g3")
                allgather(gin3, gout3, gb["gin3"], gb["gout3"])
                hal3 = sb(sc, "hal3", [128, 8, 32], F32)
                h3b = Buf("hal3")
                kb.dma("sp", hal3[:], gout3.ap().rearrange("(r p) n -> p r n", p=128), [gb["gout3"]], [h3b], sem="g3l")
                acc3 = sb(sc, "acc3", [128, 32], F32)
                kb.emit("dve", lambda e: e.tensor_scalar(out=acc3[:], in0=hal3[:, 0, :], scalar1=cst[:, C_SEL:C_SEL + 1], scalar2=None, op0=ALU.mult), [h3b, cstb], [p3b])
                for r in range(1, 8):
                    kb.emit("dve", lambda e: e.scalar_tensor_tensor(out=acc3[:], in0=hal3[:, r, :], scalar=cst[:, C_SEL + r:C_SEL + r + 1], in1=acc3[:], op0=ALU.mult, op1=ALU.add), [h3b, cstb, p3b], [p3b])
                kb.emit("dve", lambda e: e.tensor_copy(out=st["xn"][:, :, 0:2], in_=acc3[:].rearrange("p (k t) -> p k t", k=16)), [p3b], st["xnb"])
                kb.barrier()
                sc.close()
            add_call(ffn_open)

            for grp in range(4):
                for i in range(11):
                    def h_up(slot, slotb, grp=grp, i=i):
                        c = grp * 11 + i
                        par = c % 2
                        for which, off, ut, utb, cc in (("g", 0, st["ug"][par], st["ugb"][par], c), ("v", 128, st["uv"][par], st["uvb"][par], 44 + c)):
                            for tt in range(2):
                                ps, pb = kb.next_ps()
                                for k in range(16):
                                    kb.mm(ps[:], slot[:, k, off:off + 128], st["xn"][:, k, 2 + tt * 512:2 + (tt + 1) * 512], k == 0, k == 15, [slotb, st["xnb"][k]], [pb], inc=(k == 15))
                                act_copy(ut[:, 2 + tt * 512:2 + (tt + 1) * 512], ps[:], [pb], [utb])
                            ps, pb = kb.next_ps()
                            for k in range(16):
                                kb.mm(ps[:, 0:2], slot[:, k, off:off + 128], st["xn"][:, k, 0:2], k == 0, k == 15, [slotb, st["xnb"][k]], [pb], inc=(k == 15))
                            act_copy(ut[:, 0:2], ps[:, 0:2], [pb], [utb])
                        ug, ugb, uv, uvb = st["ug"][par], st["ugb"][par], st["uv"][par], st["uvb"][par]
                        wc = lambda cc, tap: pp[:, P_FCONV + cc * 3 + tap:P_FCONV + cc * 3 + tap + 1]
                        bc = lambda cc: pp[:, P_FB + cc:P_FB + cc + 1]
                        for tt in range(2):
                            t0 = tt * 512
                            for (u_, ub_, cc, dst, dstb) in ((ug, ugb, c, st["cg"], st["cgb"]), (uv, uvb, 44 + c, st["cv"], st["cvb"])):
                                kb.emit("dve", lambda e: e.tensor_scalar(out=dst[:], in0=u_[:, t0:t0 + 512], scalar1=wc(cc, 0), scalar2=bc(cc), op0=ALU.mult, op1=ALU.add), [ub_, ppb], [dstb])
                                kb.emit("dve", lambda e: e.scalar_tensor_tensor(out=dst[:], in0=u_[:, t0 + 1:t0 + 513], scalar=wc(cc, 1), in1=dst[:], op0=ALU.mult, op1=ALU.add), [ub_, ppb, dstb], [dstb])
                                kb.emit("dve", lambda e: e.scalar_tensor_tensor(out=dst[:], in0=u_[:, t0 + 2:t0 + 514], scalar=wc(cc, 2), in1=dst[:], op0=ALU.mult, op1=ALU.add), [ub_, ppb, dstb], [dstb])
                            kb.emit("act", lambda e: e.activation(out=st["sg"][:], in_=st["cg"][:], func=AF.Silu), [st["cgb"]], [st["sgb"]])
                            kb.emit("dve", lambda e: e.tensor_tensor(out=st["act"][:, i, t0:t0 + 512], in0=st["sg"][:], in1=st["cv"][:], op=ALU.mult), [st["sgb"], st["cvb"]], [st["actb"][i]])
                    c = grp * 11 + i
                    add_job([(0, 16, 0, 128, wsrc(wup_d, l, 0, 16, c * 128, 128)), (0, 16, 128, 128, wsrc(wup_d, l, 0, 16, DFF + c * 128, 128))], h_up)

                def act_rhs(k, tt):
                    return st["act"][:, k, tt * 512:(tt + 1) * 512], st["actb"][k]
                for nn in range(8):
                    def h_dn(slot, slotb, nn=nn):
                        proj_fm(slot, slotb, [(0, 128), (128, 128)], act_rhs, 11, resid_ep(nn * 2))
                    add_job([(0, 11, 0, 256, wsrc(wdn_d, l, grp * 11, 11, nn * 256, 256))], h_dn)

            def ffn_close():
                kb.barrier()
                st["ffn_o"].close()
            add_call(ffn_close)

        for l in range(NL):
            layer_steps(l)

        def final():
            sc = contextlib.ExitStack()
            sq = [sb(sc, "fsq%d" % i, [128, 512], BF16) for i in range(2)]
            sqb = [Buf("fsq%d" % i) for i in range(2)]
            rs = sb(sc, "frs", [128, T], F32)
            rsb = Buf("frs")
            for tt in range(2):
                ts = slice(tt * 512, (tt + 1) * 512)
                ps, pb = kb.next_ps()
                for k in range(16):
                    kb.emit("act", lambda e: e.activation(out=sq[k % 2][:], in_=hT[:, k, ts], func=AF.Square), [hTb[k]], [sqb[k % 2]])
                    kb.mm(ps[:], ones_bf[:], sq[k % 2][:], k == 0, k == 15, [sqb[k % 2], cbf], [pb])
                kb.emit("dve", lambda e: e.tensor_scalar(out=rs[:, ts], in0=ps[:], scalar1=1.0 / D, scalar2=EPS, op0=ALU.mult, op1=ALU.add), [pb], [rsb])
                kb.emit("act", lambda e: e.activation(out=rs[:, ts], in_=rs[:, ts], func=AF.Sqrt), [rsb], [rsb])
                kb.emit("dve", lambda e: e.reciprocal(out=rs[:, ts], in_=rs[:, ts]), [rsb], [rsb])
            yn = [sb(sc, "yn%d" % i, [128, 4, 128], F32) for i in range(2)]
            ynb = [Buf("yn%d" % i) for i in range(2)]
            ot = [sb(sc, "ot%d" % i, [128, D], F32) for i in range(2)]
            otb = [Buf("ot%d" % i) for i in range(2)]
            n = 0
            for tb in range(8):
                cs = slice(tb * 128, (tb + 1) * 128)
                for g4 in range(4):
                    y, yb = yn[n % 2], ynb[n % 2]
                    n += 1
                    for i in range(4):
                        k = g4 * 4 + i
                        kb.emit("dve", lambda e: e.scalar_tensor_tensor(out=y[:, i, :], in0=hT[:, k, cs], scalar=cst[:, C_GFIN + k:C_GFIN + k + 1], in1=rs[:, cs], op0=ALU.mult, op1=ALU.mult),
                                [hTb[k], rsb, cstb], [yb])
                    ps, pb = kb.next_ps()
                    for i in range(4):
                        kb.emit("pe", lambda e: e.transpose(ps[:, i * 128:(i + 1) * 128], y[:, i, :], ident), [yb, cstb], [pb])
                    act_copy(ot[tb % 2][:, g4 * 512:(g4 + 1) * 512], ps[:], [pb], [otb[tb % 2]])
                kb.dma("sp", out_d[tb * 128:(tb + 1) * 128, :], ot[tb % 2][:], [otb[tb % 2]], [], sem="out%d" % (tb % 2))
            for i in range(2):
                h, c = kb.dsems["out%d" % i]
                nc.sync.wait_ge(h, c)
            if "dbg" in kb.dsems:
                h, c = kb.dsems["dbg"]
                nc.sync.wait_ge(h, c)
            kb.barrier()
            sc.close()

        if CUT is not None:
            steps = steps[:CUT]

            def cleanup():
                kb.barrier()
                for k_ in ("swa", "gla2", "gla", "xa_i", "ffn_o", "xa_o", "mx_o"):
                    if k_ in st:
                        st[k_].close()
            add_call(cleanup)
        add_call(final)
        norm_steps = []
        for s_ in steps:
            if s_[0] == "job":
                wstate["jobs"].append(s_[1])
                norm_steps.append(("job", s_[2]))
            else:
                norm_steps.append(s_)
        for s_ in norm_steps:
            if s_[0] == "job":
                slot, slotb = get_slot()
                s_[1](slot, slotb)
            else:
                s_[1]()
    return nc, dbg_map


def host_consts(core):
    c = np.zeros((128, NCST), np.float32)
    i = np.arange(128)
    c[:, C_IDENT:C_IDENT + 128] = np.eye(128, dtype=np.float32)
    c[:, C_TRI:C_TRI + 128] = np.where(i[:, None] <= i[None, :], -1.0 / 16.0, 0.0)
    c[:, C_LE:C_LE + 128] = (i[:, None] <= i[None, :]).astype(np.float32)
    c[:, C_GT:C_GT + 128] = (i[:, None] > i[None, :]).astype(np.float32)
    P = np.zeros((128, 128), np.float32)
    for hb in range(2):
        for dd in range(32):
            P[hb * 64 + dd, hb * 64 + dd + 32] = -1.0
            P[hb * 64 + dd + 32, hb * 64 + dd] = 1.0
    c[:, C_ROTP:C_ROTP + 128] = P.T
    c[:, C_ONES:C_ONES + 128] = 1.0
    inv = 1.0 / (10000.0 ** (np.arange(0, 64, 2, dtype=np.float32) / 64.0))
    c[:, C_INVF] = inv[(i % 64) % 32]
    if core > 0:
        c[:, C_SEL + core - 1] = 1.0
        c[:, C_HASPREV] = 1.0
    c[:, C_LT:C_LT + 8] = (np.arange(8) < core).astype(np.float32)[None, :]
    return c


SIM_HOOK = None


def _run(inputs, NL=DEPTH, dbg_spec=None, stop_after=None):
    f = lambda a: np.ascontiguousarray(np.asarray(a))
    x = f(inputs["x"])[0]
    mem = f(inputs["mem"])[0]
    pos = f(inputs["positions"])[0].astype(np.int32)
    pp = np.zeros((DEPTH, 128, NPP), np.float32)
    for l in range(DEPTH):
        for nm, off in (("norm_mix", P_GMIX), ("norm_x", P_GX), ("norm_mem", P_GMEM), ("norm_ffn", P_GFFN)):
            pp[l, :, off:off + 16] = f(inputs[nm])[l].reshape(16, 128).T
        pp[l, :, P_GLAN] = f(inputs["gla_norm"])[l]
        pp[l, :, P_SCONV:P_SCONV + 12] = f(inputs["sc_conv"])[l].reshape(3, 4, 128).transpose(2, 1, 0).reshape(128, 12)
        pp[l, :, P_FCONV:P_FCONV + 264] = f(inputs["ffn_conv"])[l].reshape(3, 88, 128).transpose(2, 1, 0).reshape(128, 264)
        pp[l, :, P_FB:P_FB + 88] = f(inputs["ffn_conv_b"])[l].reshape(88, 128).T
        sk = f(inputs["swa_sinks"])[l]
        pp[l, 0:64, P_SINK:P_SINK + 8] = sk[None, 0:8]
        pp[l, 64:128, P_SINK:P_SINK + 8] = sk[None, 8:16]
    wg = np.concatenate([f(inputs["gla_w_gate"]), f(inputs["gla_b_gate"])[:, None, :]], axis=1).astype(np.float32)
    gfin = f(inputs["norm_final"]).reshape(16, 128).T
    shared = {k: f(inputs[k])[:NL] for k in needed_weights(NL, stop_after)}
    in_maps = []
    for c in range(NCORE):
        cst = host_consts(c)
        cst[:, C_GFIN:C_GFIN + 16] = gfin
        m = dict(shared)
        m.update({"x": np.ascontiguousarray(x[c * T:(c + 1) * T]), "mem": mem,
                  "pos": np.ascontiguousarray(np.broadcast_to(pos[c * T:(c + 1) * T][None, :], (128, T))),
                  "cst": cst, "pp": pp, "wg": wg})
        in_maps.append(m)
    nc, dbg_map = build(NL, dbg_spec, stop_after)
    if SIM_HOOK is not None:
        return SIM_HOOK(nc, in_maps), None, dbg_map
    res = run_bass_kernel_spmd(nc, in_maps, core_ids=list(range(NCORE)))
    out = np.concatenate([res.results[c]["out"] for c in range(NCORE)], axis=0)[None]
    return out.astype(np.float32), res, dbg_map


def kernel(**inputs):
    out, _, _ = _run(inputs)
    return out
```

```python
import contextlib
import math
import numpy as np
import concourse.bass as bass
import concourse.mybir as mybir
from concourse.bass_utils import run_bass_kernel_spmd

F32 = mybir.dt.float32
BF16 = mybir.dt.bfloat16
I32 = mybir.dt.int32
AF = mybir.ActivationFunctionType
ALU = mybir.AluOpType

D = 2048
T = 1024
NCORE = 8
DEPTH = 4
NIN = 4368
DFF = 5632
NMEM = 256
EPS = 1e-6
NSLOT = 4
WCOLS = 256
SAME_SYNC = True
DOUBLE_CC = True

C_IDENT, C_TRI, C_LE, C_GT, C_ROTP, C_ONES = 0, 128, 256, 384, 512, 640
C_INVF, C_SEL, C_LT, C_HASPREV, C_GFIN = 768, 769, 777, 785, 786
NCST = 802
P_GMIX, P_GX, P_GMEM, P_GFFN, P_GLAN, P_SCONV, P_FCONV, P_FB, P_SINK = 0, 16, 32, 48, 64, 65, 77, 341, 429
NPP = 445
O_GQ, O_GK, O_GV, O_GR, O_LR, O_SQ, O_SK, O_SV, O_CB, O_CC, O_CH = 0, 256, 512, 1024, 1536, 1552, 2576, 2704, 2832, 3344, 3856


class Buf:
    __slots__ = ("name", "w", "r", "excl")

    def __init__(self, name, excl=False):
        self.name = name
        self.w = None
        self.r = {}
        self.excl = excl


class KB:
    def __init__(self, nc, es):
        self.nc = nc
        self.es = es
        self.eng = {"pe": nc.tensor, "act": nc.scalar, "dve": nc.vector, "pool": nc.gpsimd, "sp": nc.sync}
        self.sem = {e: es.enter_context(nc.semaphore("c_" + e)) for e in ("pe", "act", "dve", "pool")}
        self.cnt = {e: 0 for e in self.sem}
        self.waited = {}
        self.dsems = {}
        self.ps = []
        self.psb = []
        self.psi = 0

    def _deps(self, eng, reads, writes):
        deps = {}

        def add(ev):
            if ev is None:
                return
            key, h, v = ev
            if key not in deps or deps[key][1] < v:
                deps[key] = (h, v)

        for b in reads:
            add(b.w)
            if b.excl:
                for ev in b.r.values():
                    add(ev)
        for b in writes:
            add(b.w)
            for ev in b.r.values():
                add(ev)
        e = self.eng[eng]
        for key, (h, v) in deps.items():
            if key == eng and (eng == "pe" or not SAME_SYNC):
                continue
            if self.waited.get((eng, key), 0) >= v:
                continue
            e.wait_ge(h, v)
            self.waited[(eng, key)] = v

    def _post(self, ev, reads, writes):
        for b in reads:
            old = b.r.get(ev[0])
            if old is None or old[2] < ev[2]:
                b.r[ev[0]] = ev
        for b in writes:
            b.w = ev
            b.r = {}

    def emit(self, eng, fn, reads=(), writes=(), inc=True):
        self._deps(eng, reads, writes)
        inst = fn(self.eng[eng])
        if inc:
            self.cnt[eng] += 1
            inst.then_inc(self.sem[eng], 1)
            ev = (eng, self.sem[eng], self.cnt[eng])
        else:
            ev = (eng, self.sem[eng], self.cnt[eng] + 1)
        self._post(ev, reads, writes)
        return inst

    def dma(self, q, out, in_, reads=(), writes=(), sem="d"):
        self._deps(q, reads, writes)
        if sem not in self.dsems:
            self.dsems[sem] = [self.es.enter_context(self.nc.semaphore("d_" + sem)), 0]
        ent = self.dsems[sem]
        ent[1] += 16
        self.eng[q].dma_start(out=out, in_=in_).then_inc(ent[0], 16)
        ev = ("dma:" + sem, ent[0], ent[1])
        self._post(ev, reads, writes)

    def mm(self, out, lhsT, rhs, start, stop, reads, writes, inc=True):
        return self.emit("pe", lambda e: e.matmul(out, lhsT=lhsT, rhs=rhs, start=start, stop=stop), reads, writes, inc)

    def next_ps(self):
        i = self.psi % len(self.ps)
        self.psi += 1
        return self.ps[i], self.psb[i]

    def barrier(self):
        for e in ("pe", "act", "dve", "pool", "sp"):
            for k in ("pe", "act", "dve"):
                if k == e:
                    continue
                v = self.cnt[k]
                if v > 0 and self.waited.get((e, k), 0) < v:
                    self.eng[e].wait_ge(self.sem[k], v)
                    self.waited[(e, k)] = v


CUT = None


def needed_weights(NL, stop_after):
    if NL == 0:
        return []
    n = ["w_in", "w_out"]
    if stop_after == "mixer" and NL == 1:
        return n
    n += ["xa_wq", "xa_wk", "xa_wv", "xa_wo"]
    if stop_after == "xa" and NL == 1:
        return n
    return n + ["ffn_w_up", "ffn_w_down"]


def build(NL=DEPTH, dbg_spec=None, stop_after=None):
    nc = bass.Bass("TRN2", target_bir_lowering=False)

    def din(name, shape, d=F32):
        return nc.dram_tensor(name, shape, d, kind="ExternalInput").ap()

    x_d = din("x", [T, D])
    mem_d = din("mem", [NMEM, D])
    pos_d = din("pos", [128, T], I32)
    cst_d = din("cst", [128, NCST])
    pp_d = din("pp", [DEPTH, 128, NPP])
    wg_d = din("wg", [DEPTH, 17, 256])
    need = needed_weights(NL, stop_after)
    w_in = din("w_in", [NL, D, NIN]) if "w_in" in need else None
    w_out = din("w_out", [NL, D, D]) if "w_out" in need else None
    wq_d = din("xa_wq", [NL, D, D]) if "xa_wq" in need else None
    wk_d = din("xa_wk", [NL, D, D]) if "xa_wk" in need else None
    wv_d = din("xa_wv", [NL, D, D]) if "xa_wv" in need else None
    wo_d = din("xa_wo", [NL, D, D]) if "xa_wo" in need else None
    wup_d = din("ffn_w_up", [NL, D, 2 * DFF]) if "ffn_w_up" in need else None
    wdn_d = din("ffn_w_down", [NL, DFF, D]) if "ffn_w_down" in need else None
    out_d = nc.dram_tensor("out", [T, D], F32, kind="ExternalOutput").ap()
    gin1 = nc.dram_tensor("gin1", [128, 514], F32)
    gout1 = nc.dram_tensor("gout1", [1024, 514], F32)
    gin2 = nc.dram_tensor("gin2", [128, 264], F32)
    gout2 = nc.dram_tensor("gout2", [1024, 264], F32)
    gin3 = nc.dram_tensor("gin3", [128, 32], F32)
    gout3 = nc.dram_tensor("gout3", [1024, 32], F32)
    gb = {n: Buf(n) for n in ("gin1", "gout1", "gin2", "gout2", "gin3", "gout3")}
    dbg_d = None
    if dbg_spec:
        dbg_d = nc.dram_tensor("dbg", [128, dbg_spec], F32, kind="ExternalOutput").ap()
    dbg_off = [0]
    dbg_map = {}

    es = contextlib.ExitStack()
    with es:
        kb = KB(nc, es)

        uid = [0]

        def sb(stack, name, shape, d):
            uid[0] += 1
            return stack.enter_context(nc.sbuf_tensor("%s_u%d" % (name, uid[0]), shape, d))

        for i in range(8):
            kb.ps.append(es.enter_context(nc.psum_tensor("ps%d" % i, [128, 512], F32)))
            kb.psb.append(Buf("ps%d" % i, excl=True))

        hT = sb(es, "hT", [128, 16, T], F32)
        hTb = [Buf("hT%d" % k) for k in range(16)]
        cst = sb(es, "cst", [128, NCST], F32)
        cstb = Buf("cst")
        ones_bf = sb(es, "ones_bf", [128, 128], BF16)
        ident_bf = sb(es, "ident_bf", [128, 128], BF16)
        le_bf = sb(es, "le_bf", [128, 128], BF16)
        gt_bf = sb(es, "gt_bf", [128, 128], BF16)
        cbf = Buf("cbf")
        cosT = sb(es, "cosT", [128, T], BF16)
        sinT = sb(es, "sinT", [128, T], BF16)
        csb = Buf("cossin")
        pp = sb(es, "pp", [128, NPP], F32)
        ppb = Buf("pp")
        wgf = sb(es, "wgf", [17, 256], F32)
        wgb = sb(es, "wgb", [17, 256], BF16)
        wgbuf = Buf("wg")
        wslots = [sb(es, "wsl%d" % i, [128, 16, WCOLS], BF16) for i in range(NSLOT)]
        wslotb = [Buf("wsl%d" % i) for i in range(NSLOT)]

        ident = cst[:, C_IDENT:C_IDENT + 128]
        tri = cst[:, C_TRI:C_TRI + 128]
        rotP = cst[:, C_ROTP:C_ROTP + 128]
        ones_f = cst[:, C_ONES:C_ONES + 128]

        def dump(name, ap, bufs, ncols):
            if dbg_d is None:
                return
            o = dbg_off[0]
            dbg_map[name] = (o, ncols, ap.shape[0])
            kb.dma("pool", dbg_d[0:ap.shape[0], o:o + ncols], ap, reads=bufs, writes=[], sem="dbg")
            dbg_off[0] += ncols

        steps = []
        wstate = {"issued": 0, "consumed": 0, "jobs": []}

        def wsrc(W, l, k0, nk, c0, ncol):
            return W[l, k0 * 128:(k0 + nk) * 128, c0:c0 + ncol].rearrange("(k p) n -> p k n", p=128)

        def add_job(parts, handler):
            steps.append(("job", parts, handler))

        def add_call(fn):
            steps.append(("call", fn))

        def issue_job(ji):
            parts = wstate["jobs"][ji]
            si = ji % NSLOT
            for part in parts:
                k0, nk, c0, ncol, src = part[:5]
                if len(part) > 5:
                    p0, pn = part[5], part[6]
                    kb.dma("pool", wslots[si][p0:p0 + pn, k0, c0:c0 + ncol], src, reads=[], writes=[wslotb[si]], sem="w%d" % si)
                else:
                    kb.dma("pool", wslots[si][:, k0:k0 + nk, c0:c0 + ncol], src, reads=[], writes=[wslotb[si]], sem="w%d" % si)

        def get_slot():
            idx = wstate["consumed"]
            wstate["consumed"] += 1
            while wstate["issued"] < min(len(wstate["jobs"]), idx + NSLOT):
                issue_job(wstate["issued"])
                wstate["issued"] += 1
            return wslots[idx % NSLOT], wslotb[idx % NSLOT]

        st = {}

        def act_copy(out, in_, reads, writes, scale=None):
            if scale is None:
                kb.emit("act", lambda e: e.activation(out=out, in_=in_, func=AF.Copy), reads, writes)
            else:
                kb.emit("act", lambda e: e.activation(out=out, in_=in_, func=AF.Copy, scale=scale), reads, writes)

        def rmsnorm(gcol0, dst, dstb, dst_off, sc):
            sq = [sb(sc, "sq%d" % i, [128, 512], BF16) for i in range(2)]
            sqb = [Buf("sq%d" % i) for i in range(2)]
            rs = sb(sc, "rs", [128, 512], F32)
            rsb = Buf("rs")
            for tt in range(2):
                ts = slice(tt * 512, (tt + 1) * 512)
                ps, pb = kb.next_ps()
                for k in range(16):
                    kb.emit("act", lambda e: e.activation(out=sq[k % 2][:], in_=hT[:, k, ts], func=AF.Square), [hTb[k]], [sqb[k % 2]])
                    kb.mm(ps[:], ones_bf[:], sq[k % 2][:], k == 0, k == 15, [sqb[k % 2], cbf], [pb])
                kb.emit("dve", lambda e: e.tensor_scalar(out=rs[:], in0=ps[:], scalar1=1.0 / D, scalar2=EPS, op0=ALU.mult, op1=ALU.add), [pb], [rsb])
                kb.emit("act", lambda e: e.activation(out=rs[:], in_=rs[:], func=AF.Sqrt), [rsb], [rsb])
                kb.emit("dve", lambda e: e.reciprocal(out=rs[:], in_=rs[:]), [rsb], [rsb])
                for k in range(16):
                    kb.emit("dve", lambda e: e.scalar_tensor_tensor(out=dst[:, k, dst_off + tt * 512:dst_off + (tt + 1) * 512], in0=hT[:, k, ts],
                                                                   scalar=pp[:, gcol0 + k:gcol0 + k + 1], in1=rs[:], op0=ALU.mult, op1=ALU.mult),
                            [hTb[k], rsb, ppb], [dstb[k]])

        def proj_fm(slot, slotb, ccols, rhs_fn, nk, epilogue, ntt=2):
            for ci, (c0, m) in enumerate(ccols):
                for tt in range(ntt):
                    ps, pb = kb.next_ps()
                    for k in range(nk):
                        r, rb = rhs_fn(k, tt)
                        kb.mm(ps[0:m, 0:r.shape[-1]], slot[:, k, c0:c0 + m], r, k == 0, k == nk - 1, [slotb, rb], [pb], inc=(k == nk - 1))
                    epilogue(ci, tt, ps, pb)

        def proj_tm(slot, slotb, c0, ncols, lhs_fn, nk, epilogue, ntb=8):
            for tb in range(ntb):
                ps, pb = kb.next_ps()
                for k in range(nk):
                    l, lb = lhs_fn(k, tb)
                    kb.mm(ps[0:l.shape[-1], 0:ncols], l, slot[:, k, c0:c0 + ncols], k == 0, k == nk - 1, [slotb, lb], [pb], inc=(k == nk - 1))
                epilogue(tb, ps, pb)

        gdi = nc.dram_tensor("gdi", [16, 16], F32)
        gdo = nc.dram_tensor("gdo", [128, 16], F32)
        gdb = Buf("gd")

        def allgather(gi, go, gib, gob):
            kb.emit("pool", lambda e: e.collective_compute("AllGather", ALU.bypass, replica_groups=[list(range(NCORE))],
                                                            ins=[gi.ap().opt()], outs=[go.ap().opt()]), [gib], [gob])
            if DOUBLE_CC:
                kb.emit("pool", lambda e: e.collective_compute("AllGather", ALU.bypass, replica_groups=[list(range(NCORE))],
                                                                ins=[gdi.ap().opt()], outs=[gdo.ap().opt()]), [gob, gib, gdb], [gob, gib, gdb])

        def setup():
            sc = contextlib.ExitStack()
            kb.dma("sp", cst[:], cst_d[:, :], [], [cstb], sem="cst")
            kb.emit("dve", lambda e: e.tensor_copy(out=ones_bf[:], in_=ones_f), [cstb], [cbf])
            kb.emit("dve", lambda e: e.tensor_copy(out=ident_bf[:], in_=ident), [cstb], [cbf])
            kb.emit("dve", lambda e: e.tensor_copy(out=le_bf[:], in_=cst[:, C_LE:C_LE + 128]), [cstb], [cbf])
            kb.emit("dve", lambda e: e.tensor_copy(out=gt_bf[:], in_=cst[:, C_GT:C_GT + 128]), [cstb], [cbf])
            posi = sb(sc, "posi", [128, T], I32)
            a = sb(sc, "ang", [128, T], F32)
            a2 = sb(sc, "ang2", [128, T], F32)
            kf = sb(sc, "kf", [128, T], F32)
            ki = sb(sc, "ki", [128, T], I32)
            tb_ = Buf("rope_tmp")
            kb.dma("sp", posi[:], pos_d[:, :], [], [tb_], sem="pos")
            kb.emit("dve", lambda e: e.tensor_copy(out=a[:], in_=posi[:]), [tb_], [tb_])
            kb.emit("dve", lambda e: e.tensor_scalar(out=a[:], in0=a[:], scalar1=cst[:, C_INVF:C_INVF + 1], scalar2=None, op0=ALU.mult), [tb_, cstb], [tb_])
            C1 = 6.28125
            C2 = 2.0 * math.pi - C1
            for which, shift, dstT in (("sin", 0.0, sinT), ("cos", math.pi / 2, cosT)):
                kb.emit("dve", lambda e: e.tensor_scalar(out=a2[:], in0=a[:], scalar1=shift, scalar2=None, op0=ALU.add), [tb_], [tb_])
                kb.emit("dve", lambda e: e.tensor_scalar(out=ki[:], in0=a2[:], scalar1=1.0 / (2 * math.pi), scalar2=None, op0=ALU.mult), [tb_], [tb_])
                kb.emit("dve", lambda e: e.tensor_copy(out=kf[:], in_=ki[:]), [tb_], [tb_])
                kb.emit("dve", lambda e: e.scalar_tensor_tensor(out=a2[:], in0=kf[:], scalar=-C1, in1=a2[:], op0=ALU.mult, op1=ALU.add), [tb_], [tb_])
                kb.emit("dve", lambda e: e.scalar_tensor_tensor(out=a2[:], in0=kf[:], scalar=-C2, in1=a2[:], op0=ALU.mult, op1=ALU.add), [tb_], [tb_])
                kb.emit("dve", lambda e: e.tensor_scalar(out=kf[:], in0=a2[:], scalar1=math.pi, scalar2=-2 * math.pi, op0=ALU.is_gt, op1=ALU.mult), [tb_], [tb_])
                kb.emit("dve", lambda e: e.tensor_tensor(out=a2[:], in0=a2[:], in1=kf[:], op=ALU.add), [tb_], [tb_])
                kb.emit("dve", lambda e: e.tensor_scalar(out=kf[:], in0=a2[:], scalar1=-math.pi, scalar2=2 * math.pi, op0=ALU.is_lt, op1=ALU.mult), [tb_], [tb_])
                kb.emit("dve", lambda e: e.tensor_tensor(out=a2[:], in0=a2[:], in1=kf[:], op=ALU.add), [tb_], [tb_])
                kb.emit("dve", lambda e: e.tensor_scalar(out=a2[:], in0=a2[:], scalar1=3.1415925, scalar2=-3.1415925, op0=ALU.min, op1=ALU.max), [tb_], [tb_])
                kb.emit("act", lambda e: e.activation(out=dstT[:], in_=a2[:], func=AF.Sin), [tb_], [tb_, csb])
            xs = [sb(sc, "xs%d" % i, [128, D], F32) for i in range(2)]
            xsb = [Buf("xs%d" % i) for i in range(2)]
            for tb in range(8):
                kb.dma("sp", xs[tb % 2][:], x_d[tb * 128:(tb + 1) * 128, :], [], [xsb[tb % 2]], sem="xs%d" % (tb % 2))
                for g4 in range(4):
                    ps, pb = kb.next_ps()
                    for i in range(4):
                        fc = g4 * 4 + i
                        kb.emit("pe", lambda e: e.transpose(ps[:, i * 128:(i + 1) * 128], xs[tb % 2][:, fc * 128:(fc + 1) * 128], ident), [xsb[tb % 2], cstb], [pb])
                    kb.emit("act", lambda e: e.activation(out=hT[:, g4 * 4:(g4 + 1) * 4, tb * 128:(tb + 1) * 128],
                                                           in_=ps[:].rearrange("p (c t) -> p c t", c=4), func=AF.Copy),
                            [pb], [hTb[g4 * 4 + i] for i in range(4)])
            kb.barrier()
            sc.close()

        add_call(setup)

        def layer_steps(l):
            def load_params():
                kb.dma("sp", pp[:], pp_d[l, :, :], [], [ppb], sem="pp")
                kb.dma("sp", wgf[:], wg_d[l, :, :], [], [wgbuf], sem="wg")
                kb.emit("dve", lambda e: e.tensor_copy(out=wgb[:], in_=wgf[:]), [wgbuf], [wgbuf])
            add_call(load_params)

            def mixer_open():
                o = contextlib.ExitStack()
                st["mx_o"] = o
                st["oa"] = sb(o, "oa", [128, 4, T], BF16)
                st["oab"] = [Buf("oa%d" % i) for i in range(4)]
                st["oc"] = sb(o, "oc", [128, 4, T], BF16)
                st["ocb"] = [Buf("oc%d" % i) for i in range(4)]
                st["xn"] = sb(o, "xn", [128, 16, T + 2], BF16)
                st["xnb"] = [Buf("xn%d" % i) for i in range(16)]
                sc = contextlib.ExitStack()
                rmsnorm(P_GMIX, st["xn"], st["xnb"], 2, sc)
                kb.barrier()
                sc.close()
                g = contextlib.ExitStack()
                st["gla"] = g
                st["qT"] = sb(g, "qT", [128, 2, 2, T], BF16)
                st["kT"] = sb(g, "kT", [128, 2, T], BF16)
                st["kh"] = sb(g, "kh", [128, 8, 256], BF16)
                st["vt"] = sb(g, "vt", [128, 8, 512], BF16)
                st["blast"] = sb(g, "blast", [128, 2, 8], F32)
                st["eblast"] = sb(g, "eblast", [128, 2, 8], F32)
                st["pay1"] = sb(g, "pay1", [128, 514], F32)
                g2 = contextlib.ExitStack()
                st["gla2"] = g2
                st["ebT"] = sb(g2, "ebT", [128, 2, T], BF16)
                st["enbT"] = sb(g2, "enbT", [128, 2, T], BF16)
                st["lrT"] = sb(g2, "lrT", [32, T], BF16)
                for n in ("qT", "kT", "kh", "vt", "ebT", "enbT", "lrT", "blast", "eblast", "pay1"):
                    st[n + "b"] = Buf(n)
                kb.emit("dve", lambda e: e.memset(st["lrT"][:], 1.0), [], [st["lrTb"]])
                kb.emit("dve", lambda e: e.memset(st["qT"][:], 0.0), [], [st["qTb"]])
            add_call(mixer_open)

            def xn_rhs(k, tt):
                return st["xn"][:, k, 2 + tt * 512:2 + (tt + 1) * 512], st["xnb"][k]

            def xn_lhs(k, tb):
                return st["xn"][:, k, 2 + tb * 128:2 + (tb + 1) * 128], st["xnb"][k]

            def h_glr(slot, slotb):
                def ep(ci, tt, ps, pb):
                    act_copy(st["lrT"][0:16, tt * 512:(tt + 1) * 512], ps[0:16, :], [pb], [st["lrTb"]])
                proj_fm(slot, slotb, [(0, 16)], xn_rhs, 16, ep)
                import os
                SUB = int(os.environ.get("SUBCUT", "9"))
                if SUB <= 1:
                    return
                g = st["gla2"]
                ex = sb(g, "ex", [128, 256], F32)
                sp = sb(g, "sp", [128, 256], F32)
                exb, spb = Buf("ex"), Buf("sp")
                for tb in range(8):
                    ps, pb = kb.next_ps()
                    kb.mm(ps[:, 0:256], st["lrT"][0:17, tb * 128:(tb + 1) * 128], wgb[0:17, :], True, True, [st["lrTb"], wgbuf], [pb])
                    if SUB <= 2:
                        continue
                    kb.emit("act", lambda e: e.activation(out=ex[:], in_=ps[:, 0:256], func=AF.Exp, scale=-1.0), [pb], [exb])
                    kb.emit("act", lambda e: e.activation(out=sp[:], in_=ex[:], func=AF.Ln, bias=1.0), [exb], [spb])
                    if SUB <= 3:
                        continue
                    for j in range(2):
                        ps2, pb2 = kb.next_ps()
                        kb.mm(ps2[:, 0:128], sp[:, j * 128:(j + 1) * 128], tri, True, True, [spb, cstb], [pb2])
                        if SUB <= 4:
                            continue
                        cs = slice(tb * 128, (tb + 1) * 128)
                        kb.emit("act", lambda e: e.activation(out=st["ebT"][:, j, cs], in_=ps2[:, 0:128], func=AF.Exp), [pb2], [st["ebTb"]])
                        kb.emit("act", lambda e: e.activation(out=st["enbT"][:, j, cs], in_=ps2[:, 0:128], func=AF.Exp, scale=-1.0), [pb2], [st["enbTb"]])
                        kb.emit("dve", lambda e: e.tensor_copy(out=st["blast"][:, j, tb:tb + 1], in_=ps2[:, 127:128]), [pb2], [st["blastb"]])
                if SUB <= 5:
                    return
                kb.emit("act", lambda e: e.activation(out=st["eblast"][:], in_=st["blast"][:], func=AF.Exp), [st["blastb"]], [st["eblastb"]])
                for j in range(2):
                    kb.emit("dve", lambda e: e.tensor_reduce(out=st["pay1"][:, 512 + j:513 + j], in_=st["blast"][:, j, :], axis=mybir.AxisListType.X, op=ALU.add),
                            [st["blastb"]], [st["pay1b"]])
            add_job([(0, 16, 0, 16, wsrc(w_in, l, 0, 16, O_LR, 16))], h_glr)

            def h_gq(slot, slotb):
                def ep(ci, tt, ps, pb):
                    ts = slice(tt * 512, (tt + 1) * 512)
                    for hp in range(2):
                        hs = slice(hp * 64, (hp + 1) * 64)
                        kb.emit("dve", lambda e: e.scalar_tensor_tensor(out=st["qT"][hs, hp, ci, ts], in0=ps[hs, :], scalar=0.125, in1=st["ebT"][hs, ci, ts], op0=ALU.mult, op1=ALU.mult),
                                [pb, st["ebTb"]], [st["qTb"]])
                proj_fm(slot, slotb, [(0, 128), (128, 128)], xn_rhs, 16, ep)
            add_job([(0, 16, 0, 256, wsrc(w_in, l, 0, 16, O_GQ, 256))], h_gq)

            def h_gk(slot, slotb):
                def ep(ci, tt, ps, pb):
                    ts = slice(tt * 512, (tt + 1) * 512)
                    kb.emit("dve", lambda e: e.tensor_tensor(out=st["kT"][:, ci, ts], in0=ps[:], in1=st["enbT"][:, ci, ts], op=ALU.mult), [pb, st["enbTb"]], [st["kTb"]])
                proj_fm(slot, slotb, [(0, 128), (128, 128)], xn_rhs, 16, ep)
                g = st["gla2"]
                ktmp = [sb(g, "ktmp%d" % i, [128, 128], BF16) for i in range(2)]
                ktb = [Buf("ktmp%d" % i) for i in range(2)]
                n = 0
                for j in range(2):
                    for tb in range(8):
                        kt, ktbb = ktmp[n % 2], ktb[n % 2]
                        n += 1
                        kb.emit("dve", lambda e: e.tensor_scalar(out=kt[:], in0=st["kT"][:, j, tb * 128:(tb + 1) * 128], scalar1=st["eblast"][:, j, tb:tb + 1], scalar2=None, op0=ALU.mult),
                                [st["kTb"], st["eblastb"]], [ktbb])
                        ps, pb = kb.next_ps()
                        pst = ps[:, 0:64].bitcast(BF16)
                        kb.emit("pe", lambda e: e.transpose(pst, kt[:], ident_bf[:]), [ktbb, cbf], [pb])
                        act_copy(st["kh"][:, tb, j * 128:(j + 1) * 128], pst, [pb], [st["khb"]])
            add_job([(0, 16, 0, 256, wsrc(w_in, l, 0, 16, O_GK, 256))], h_gk)

            for half in range(2):
                def h_gv(slot, slotb, half=half):
                    def ep(tb, ps, pb):
                        act_copy(st["vt"][:, tb, half * 256:(half + 1) * 256], ps[:, 0:256], [pb], [st["vtb"]])
                    proj_tm(slot, slotb, 0, 256, xn_lhs, 16, ep)
                add_job([(0, 16, 0, 256, wsrc(w_in, l, 0, 16, O_GV + half * 256, 256))], h_gv)

            for half in range(2):
                def h_gr(slot, slotb, half=half):
                    def ep(ci, tt, ps, pb):
                        kb.emit("act", lambda e: e.activation(out=st["oa"][:, half * 2 + ci, tt * 512:(tt + 1) * 512], in_=ps[:], func=AF.Silu), [pb], [st["oab"][half * 2 + ci]])
                    proj_fm(slot, slotb, [(0, 128), (128, 128)], xn_rhs, 16, ep)
                add_job([(0, 16, 0, 256, wsrc(w_in, l, 0, 16, O_GR + half * 256, 256))], h_gr)

            def gla_core():
                kb.barrier()
                st["gla2"].close()
                g = st["gla"]
                S = st["pay1"][:, 0:512].rearrange("p (j e) -> p j e", j=2)
                Sb = st["pay1b"]
                kb.emit("dve", lambda e: e.memset(st["pay1"][:, 0:512], 0.0), [], [Sb])

                def kv_mm(tb, j):
                    ps, pb = kb.next_ps()
                    kb.mm(ps[:, 0:256], st["kh"][:, tb, j * 128:(j + 1) * 128], st["vt"][:, tb, j * 256:(j + 1) * 256], True, True, [st["khb"], st["vtb"]], [pb])
                    return ps, pb
                for tb in range(8):
                    for j in range(2):
                        ps, pb = kv_mm(tb, j)
                        kb.emit("dve", lambda e: e.scalar_tensor_tensor(out=S[:, j, :], in0=S[:, j, :], scalar=st["eblast"][:, j, tb:tb + 1], in1=ps[:, 0:256], op0=ALU.mult, op1=ALU.add),
                                [pb, st["eblastb"]], [Sb])
                import os
                SUB = int(os.environ.get("SUBG", "9"))
                if SUB <= 1:
                    return
                kb.dma("sp", gin1.ap()[:, :], st["pay1"][:], [Sb], [gb["gin1"]], sem="g1")
                allgather(gin1, gout1, gb["gin1"], gb["gout1"])
                if SUB <= 2:
                    kb.emit("dve", lambda e: e.memset(st["pay1"][:, 0:2], 0.0), [gb["gout1"]], [Sb])
                    return
                lt = cst[:, C_LT:C_LT + 8]
                suf = sb(g, "suf", [128, 2], F32)
                Mr = sb(g, "Mr", [128, 2], F32)
                Mb = Buf("Mw")
                kb.emit("dve", lambda e: e.memset(suf[:], 0.0), [], [Mb])
                Sin = sb(g, "Sin", [128, 2, 256], F32)
                Sinb = Buf("Sin")
                kb.emit("dve", lambda e: e.memset(Sin[:], 0.0), [], [Sinb])
                Sall = sb(g, "Sall", [128, 8, 514], F32)
                srb = Buf("Sall")
                kb.dma("sp", Sall[:], gout1.ap().rearrange("(r p) n -> p r n", p=128), [gb["gout1"]], [srb], sem="sst")
                for r in range(7, -1, -1):
                    sr = Sall[:, r, :]
                    VV = os.environ.get("VV", "")
                    if "a" not in VV:
                        kb.emit("act", lambda e: e.activation(out=Mr[:], in_=suf[:], func=AF.Exp), [Mb], [Mb])
                    else:
                        kb.emit("dve", lambda e: e.memset(Mr[:], 1.0), [], [Mb])
                    kb.emit("dve", lambda e: e.tensor_scalar(out=Mr[:], in0=Mr[:], scalar1=lt[:, r:r + 1], scalar2=None, op0=ALU.mult), [Mb, cstb], [Mb])
                    if "b" not in VV:
                        for j in range(2):
                            kb.emit("dve", lambda e: e.scalar_tensor_tensor(out=Sin[:, j, :], in0=sr[:, j * 256:(j + 1) * 256], scalar=Mr[:, j:j + 1], in1=Sin[:, j, :], op0=ALU.mult, op1=ALU.add),
                                    [srb, Mb], [Sinb])
                    if "c" not in VV:
                        kb.emit("dve", lambda e: e.scalar_tensor_tensor(out=suf[:], in0=sr[:, 512:514], scalar=lt[:, r:r + 1], in1=suf[:], op0=ALU.mult, op1=ALU.add), [srb, Mb, cstb], [Mb])
                if SUB <= 3:
                    return
                Sbf = sb(g, "Sbf", [128, 2, 256], BF16)
                Sbfb = Buf("Sbf")
                attm = sb(g, "attm", [128, 4, 128], BF16)
                attmb = Buf("attm")
                sqo = sb(g, "sqo", [128, 512], BF16)
                sqob = Buf("sqo")
                rso = sb(g, "rso", [128, 512], F32)
                rsob = Buf("rso")
                on = sb(g, "on", [128, 512], F32)
                onb = Buf("on")
                for tb in range(8):
                    cs = slice(tb * 128, (tb + 1) * 128)
                    act_copy(Sbf[:], Sin[:], [Sinb], [Sbfb])
                    psa, pba = kb.next_ps()
                    for h in range(4):
                        j, hp = h // 2, h % 2
                        kb.mm(psa[:, h * 128:(h + 1) * 128], st["kT"][:, j, cs], st["qT"][:, hp, j, cs], True, True, [st["kTb"], st["qTb"]], [pba])
                    kb.emit("dve", lambda e: e.tensor_tensor(out=attm[:], in0=psa[:].rearrange("p (h t) -> p h t", h=4), in1=le_bf[:].unsqueeze(1).broadcast_to([128, 4, 128]), op=ALU.mult),
                            [pba, cbf], [attmb])
                    pso, pbo = kb.next_ps()
                    for h in range(4):
                        j, hp = h // 2, h % 2
                        kb.mm(pso[:, h * 128:(h + 1) * 128], st["vt"][:, tb, h * 128:(h + 1) * 128], attm[:, h, :], True, False, [st["vtb"], attmb], [pbo], inc=False)
                        kb.mm(pso[:, h * 128:(h + 1) * 128], Sbf[:, j, hp * 128:(hp + 1) * 128], st["qT"][:, hp, j, cs], False, True, [Sbfb, st["qTb"]], [pbo])
                    kb.emit("act", lambda e: e.activation(out=sqo[:], in_=pso[:], func=AF.Square), [pbo], [sqob])
                    psn, pbn = kb.next_ps()
                    kb.mm(psn[:], ones_bf[:], sqo[:], True, True, [sqob, cbf], [pbn])
                    kb.emit("dve", lambda e: e.tensor_scalar(out=rso[:], in0=psn[:], scalar1=1.0 / 128, scalar2=EPS, op0=ALU.mult, op1=ALU.add), [pbn], [rsob])
                    kb.emit("act", lambda e: e.activation(out=rso[:], in_=rso[:], func=AF.Sqrt), [rsob], [rsob])
                    kb.emit("dve", lambda e: e.reciprocal(out=rso[:], in_=rso[:]), [rsob], [rsob])
                    kb.emit("dve", lambda e: e.tensor_tensor(out=on[:], in0=pso[:], in1=rso[:], op=ALU.mult), [pbo, rsob], [onb])
                    kb.emit("dve", lambda e: e.scalar_tensor_tensor(out=st["oa"][:, :, cs], in0=on[:].rearrange("p (h t) -> p h t", h=4), scalar=pp[:, P_GLAN:P_GLAN + 1], in1=st["oa"][:, :, cs], op0=ALU.mult, op1=ALU.mult),
                            [onb, ppb] + st["oab"], st["oab"])
                    if tb < 7:
                        for j in range(2):
                            ps, pb = kv_mm(tb, j)
                            kb.emit("dve", lambda e: e.scalar_tensor_tensor(out=Sin[:, j, :], in0=Sin[:, j, :], scalar=st["eblast"][:, j, tb:tb + 1], in1=ps[:, 0:256], op0=ALU.mult, op1=ALU.add),
                                    [pb, st["eblastb"]], [Sinb])
                kb.barrier()
                st["gla"].close()
                s = contextlib.ExitStack()
                st["swa"] = s
                st["Qr"] = sb(s, "Qr", [128, 8, T], BF16)
                st["Qrb"] = [Buf("Qr%d" % i) for i in range(8)]
                st["Kr"] = sb(s, "Kr", [128, 128 + T], BF16)
                st["Vt"] = sb(s, "Vt", [128, 9, 128], BF16)
                st["u"] = sb(s, "u", [128, 4, T + 2], BF16)
                st["cbf2"] = sb(s, "cbf2", [128, 4, 2], F32)
                st["pay2"] = sb(s, "pay2", [128, 264], F32)
                st["qf"] = sb(s, "qf", [128, 512], F32)
                st["t1"] = sb(s, "t1", [128, 512], F32)
                st["t2"] = sb(s, "t2", [128, 512], F32)
                for n in ("Kr", "Vt", "u", "cbf2", "pay2", "qf", "t1", "t2"):
                    st[n + "b"] = Buf(n)
                kb.emit("dve", lambda e: e.memset(st["u"][:, :, 0:2], 0.0), [], [st["ub"]])
            add_call(gla_core)

            def rope_ep(ps, pb, tt, outs):
                ts = slice(tt * 512, (tt + 1) * 512)
                act_copy(st["qf"][:], ps[:], [pb], [st["qfb"]])
                ps2, pb2 = kb.next_ps()
                kb.mm(ps2[:], rotP, st["qf"][:], True, True, [st["qfb"], cstb], [pb2])
                kb.emit("dve", lambda e: e.tensor_tensor(out=st["t1"][:], in0=st["qf"][:], in1=cosT[:, ts], op=ALU.mult), [st["qfb"], csb], [st["t1b"]])
                kb.emit("dve", lambda e: e.tensor_tensor(out=st["t2"][:], in0=ps2[:], in1=sinT[:, ts], op=ALU.mult), [pb2, csb], [st["t2b"]])
                for (oap, c0, c1, bufs) in outs:
                    kb.emit("dve", lambda e: e.tensor_tensor(out=oap, in0=st["t1"][:, c0:c1], in1=st["t2"][:, c0:c1], op=ALU.add), [st["t1b"], st["t2b"]], bufs)

            for m in range(4):
                def h_sq(slot, slotb, m=m):
                    def ep(ci, tt, ps, pb):
                        ch = 2 * m + ci
                        rope_ep(ps, pb, tt, [(st["Qr"][:, ch, tt * 512:(tt + 1) * 512], 0, 512, [st["Qrb"][ch]])])
                    proj_fm(slot, slotb, [(0, 128), (128, 128)], xn_rhs, 16, ep)
                parts = []
                for ci in range(2):
                    i = 2 * m + ci
                    parts.append((0, 16, ci * 128, 64, wsrc(w_in, l, 0, 16, O_SQ + 64 * i, 64)))
                    parts.append((0, 16, ci * 128 + 64, 64, wsrc(w_in, l, 0, 16, O_SQ + 64 * (8 + i), 64)))
                add_job(parts, h_sq)

            def h_skv(slot, slotb):
                def ep(ci, tt, ps, pb):
                    outs = [(st["Kr"][:, 128 + tt * 512:128 + (tt + 1) * 512], 0, 512, [st["Krb"]])]
                    if tt == 1:
                        outs.append((st["pay2"][:, 0:128], 384, 512, [st["pay2b"]]))
                    rope_ep(ps, pb, tt, outs)
                proj_fm(slot, slotb, [(0, 128)], xn_rhs, 16, ep)

                def epv(tb, ps, pb):
                    act_copy(st["Vt"][:, 1 + tb, :], ps[:, 0:128], [pb], [st["Vtb"]])
                    if tb == 7:
                        act_copy(st["pay2"][:, 128:256], ps[:, 0:128], [pb], [st["pay2b"]])
                proj_tm(slot, slotb, 128, 128, xn_lhs, 16, epv)
            add_job([(0, 16, 0, 256, wsrc(w_in, l, 0, 16, O_SK, 256))], h_skv)

            for half in range(2):
                def h_ch(slot, slotb, half=half):
                    def ep(ci, tt, ps, pb):
                        act_copy(st["u"][:, half * 2 + ci, 2 + tt * 512:2 + (tt + 1) * 512], ps[:], [pb], [st["ub"]])
                    proj_fm(slot, slotb, [(0, 128), (128, 128)], xn_rhs, 16, ep)
                add_job([(0, 16, 0, 256, wsrc(w_in, l, 0, 16, O_CH + half * 256, 256))], h_ch)
            for half in range(2):
                def h_cc(slot, slotb, half=half):
                    def ep(ci, tt, ps, pb):
                        ua = st["u"][:, half * 2 + ci, 2 + tt * 512:2 + (tt + 1) * 512]
                        kb.emit("dve", lambda e: e.tensor_tensor(out=ua, in0=ps[:], in1=ua, op=ALU.mult), [pb, st["ub"]], [st["ub"]])
                        if tt == 1:
                            j = half * 2 + ci
                            kb.emit("dve", lambda e: e.tensor_copy(out=st["pay2"][:, 256 + 2 * j:258 + 2 * j], in_=st["u"][:, j, T:T + 2]), [st["ub"]], [st["pay2b"]])
                    proj_fm(slot, slotb, [(0, 128), (128, 128)], xn_rhs, 16, ep)
                add_job([(0, 16, 0, 256, wsrc(w_in, l, 0, 16, O_CC + half * 256, 256))], h_cc)

            def conv_oc(j, cb_ap, cb_bufs, t0, n):
                cv = st["t1"][:, 0:n]
                w = lambda tap: pp[:, P_SCONV + j * 3 + tap:P_SCONV + j * 3 + tap + 1]
                kb.emit("dve", lambda e: e.tensor_scalar(out=cv, in0=st["u"][:, j, t0:t0 + n], scalar1=w(0), scalar2=None, op0=ALU.mult), [st["ub"], ppb], [st["t1b"]])
                kb.emit("dve", lambda e: e.scalar_tensor_tensor(out=cv, in0=st["u"][:, j, t0 + 1:t0 + 1 + n], scalar=w(1), in1=cv, op0=ALU.mult, op1=ALU.add), [st["ub"], ppb], [st["t1b"]])
                kb.emit("dve", lambda e: e.scalar_tensor_tensor(out=cv, in0=st["u"][:, j, t0 + 2:t0 + 2 + n], scalar=w(2), in1=cv, op0=ALU.mult, op1=ALU.add), [st["ub"], ppb], [st["t1b"]])
                kb.emit("dve", lambda e: e.tensor_tensor(out=st["oc"][:, j, t0:t0 + n], in0=cb_ap, in1=cv, op=ALU.mult), cb_bufs + [st["t1b"]], [st["ocb"][j]])

            for half in range(2):
                def h_cb(slot, slotb, half=half):
                    def ep(ci, tt, ps, pb):
                        j = half * 2 + ci
                        if tt == 0:
                            kb.emit("dve", lambda e: e.tensor_copy(out=st["cbf2"][:, j, :], in_=ps[:, 0:2]), [pb], [st["cbf2b"]])
                        conv_oc(j, ps[:], [pb], tt * 512, 512)
                    proj_fm(slot, slotb, [(0, 128), (128, 128)], xn_rhs, 16, ep)
                add_job([(0, 16, 0, 256, wsrc(w_in, l, 0, 16, O_CB + half * 256, 256))], h_cb)

            def swa_core():
                s = st["swa"]
                kb.dma("sp", gin2.ap()[:, :], st["pay2"][:], [st["pay2b"]], [gb["gin2"]], sem="g2")
                allgather(gin2, gout2, gb["gin2"], gb["gout2"])
                esk = sb(s, "esk", [128, 8], F32)
                esb = Buf("esk")
                kb.emit("act", lambda e: e.activation(out=esk[:], in_=pp[:, P_SINK:P_SINK + 8], func=AF.Exp), [ppb], [esb])
                PT = [sb(s, "PT%d" % i, [128, 512], BF16) for i in range(4)]
                PTb = [Buf("PT%d" % i) for i in range(4)]
                rd = sb(s, "rd", [128, 512], F32)
                rdb = Buf("rd")
                vo = sb(s, "vones", [128, 64], BF16)
                kb.emit("dve", lambda e: e.memset(vo[:], 1.0), [], [cbf])

                def block(n):
                    for a in range(2):
                        pso, pbo = kb.next_ps()
                        psd, pbd = kb.next_ps()
                        for g in range(2):
                            gp = slice(g * 64, (g + 1) * 64)
                            for kbi in range(2):
                                kc = slice((n + kbi) * 128, (n + kbi + 1) * 128)
                                pss, pbs = kb.next_ps()
                                kb.mm(pss[:], st["Kr"][gp, kc], st["Qr"][gp, 4 * a:4 * a + 4, n * 128:(n + 1) * 128], True, True,
                                      [st["Krb"]] + st["Qrb"][4 * a:4 * a + 4], [pbs])
                                P = PT[g * 2 + kbi]
                                Pb = PTb[g * 2 + kbi]
                                kb.emit("act", lambda e: e.activation(out=P[:], in_=pss[:], func=AF.Exp, scale=0.125), [pbs], [Pb])
                                msk = gt_bf if kbi == 0 else le_bf
                                kb.emit("dve", lambda e: e.tensor_tensor(out=P[:].rearrange("p (h t) -> p h t", h=4), in0=P[:].rearrange("p (h t) -> p h t", h=4),
                                                                         in1=msk[:].unsqueeze(1).broadcast_to([128, 4, 128]), op=ALU.mult), [Pb, cbf], [Pb])
                                if kbi == 0 and n == 0:
                                    kb.emit("dve", lambda e: e.tensor_scalar(out=P[:], in0=P[:], scalar1=cst[:, C_HASPREV:C_HASPREV + 1], scalar2=None, op0=ALU.mult), [Pb, cstb], [Pb])
                            for kbi in range(2):
                                P = PT[g * 2 + kbi]
                                Pb = PTb[g * 2 + kbi]
                                kb.mm(pso[gp, :], st["Vt"][:, n + kbi, gp], P[:], kbi == 0, kbi == 1, [st["Vtb"], Pb], [pbo])
                            for kbi in range(2):
                                P = PT[g * 2 + kbi]
                                Pb = PTb[g * 2 + kbi]
                                kb.mm(psd[gp, :], vo[:], P[:], kbi == 0, kbi == 1, [cbf, Pb], [pbd])
                        kb.emit("dve", lambda e: e.tensor_tensor(out=rd[:].rearrange("p (h t) -> p h t", h=4), in0=psd[:].rearrange("p (h t) -> p h t", h=4),
                                                                 in1=esk[:, 4 * a:4 * a + 4].unsqueeze(2).broadcast_to([128, 4, 128]), op=ALU.add), [pbd, esb], [rdb])
                        kb.emit("dve", lambda e: e.reciprocal(out=rd[:], in_=rd[:]), [rdb], [rdb])
                        kb.emit("dve", lambda e: e.tensor_tensor(out=st["Qr"][:, 4 * a:4 * a + 4, n * 128:(n + 1) * 128], in0=pso[:].rearrange("p (h t) -> p h t", h=4),
                                                                 in1=rd[:].rearrange("p (h t) -> p h t", h=4), op=ALU.mult),
                                [pbo, rdb], st["Qrb"][4 * a:4 * a + 4])
                for n in range(1, 8):
                    block(n)
                hal = sb(s, "hal", [128, 8, 264], F32)
                halb = Buf("hal")
                kb.dma("sp", hal[:], gout2.ap().rearrange("(r p) n -> p r n", p=128), [gb["gout2"]], [halb], sem="g2l")
                acc = sb(s, "hacc", [128, 264], F32)
                accb = Buf("hacc")
                kb.emit("dve", lambda e: e.tensor_scalar(out=acc[:], in0=hal[:, 0, :], scalar1=cst[:, C_SEL:C_SEL + 1], scalar2=None, op0=ALU.mult), [halb, cstb], [accb])
                for r in range(1, 8):
                    kb.emit("dve", lambda e: e.scalar_tensor_tensor(out=acc[:], in0=hal[:, r, :], scalar=cst[:, C_SEL + r:C_SEL + r + 1], in1=acc[:], op0=ALU.mult, op1=ALU.add), [halb, cstb], [accb])
                kb.emit("dve", lambda e: e.tensor_copy(out=st["Kr"][:, 0:128], in_=acc[:, 0:128]), [accb], [st["Krb"]])
                kb.emit("dve", lambda e: e.tensor_copy(out=st["Vt"][:, 0, :], in_=acc[:, 128:256]), [accb], [st["Vtb"]])
                kb.emit("dve", lambda e: e.tensor_copy(out=st["u"][:, :, 0:2], in_=acc[:, 256:264].rearrange("p (j t) -> p j t", j=4)), [accb], [st["ub"]])
                block(0)
                for j in range(4):
                    conv_oc(j, st["cbf2"][:, j, :], [st["cbf2b"]], 0, 2)
            add_call(swa_core)

            def mix_rhs(k, tt):
                ts = slice(tt * 512, (tt + 1) * 512)
                if k < 4:
                    return st["oa"][:, k, ts], st["oab"][k]
                if k < 12:
                    return st["Qr"][:, k - 4, ts], st["Qrb"][k - 4]
                return st["oc"][:, k - 12, ts], st["ocb"][k - 12]

            def resid_ep(n0):
                def ep(ci, tt, ps, pb):
                    ch = n0 + ci
                    ts = slice(tt * 512, (tt + 1) * 512)
                    kb.emit("dve", lambda e: e.tensor_tensor(out=hT[:, ch, ts], in0=ps[:], in1=hT[:, ch, ts], op=ALU.add), [pb, hTb[ch]], [hTb[ch]])
                return ep

            for nn in range(8):
                def h_wo(slot, slotb, nn=nn):
                    proj_fm(slot, slotb, [(0, 128), (128, 128)], mix_rhs, 16, resid_ep(nn * 2))
                parts = [(0, 4, 0, 256, wsrc(w_out, l, 0, 4, nn * 256, 256))]
                for i in range(8):
                    for g in range(2):
                        r0 = 512 + (g * 8 + i) * 64
                        parts.append((4 + i, 1, 0, 256, w_out[l, r0:r0 + 64, nn * 256:nn * 256 + 256], g * 64, 64))
                parts.append((12, 4, 0, 256, wsrc(w_out, l, 12, 4, nn * 256, 256)))
                add_job(parts, h_wo)

            def mixer_close():
                kb.barrier()
                st["swa"].close()
                st["mx_o"].close()
            add_call(mixer_close)
            if stop_after == "mixer" and l == NL - 1:
                return

            def xa_open():
                o = contextlib.ExitStack()
                st["xa_o"] = o
                st["xn"] = sb(o, "xn", [128, 16, T + 2], BF16)
                st["xnb"] = [Buf("xn%d" % i) for i in range(16)]
                st["memnT"] = sb(o, "memnT", [128, 16, NMEM], BF16)
                st["memnTb"] = Buf("memnT")
                sc = contextlib.ExitStack()
                rmsnorm(P_GX, st["xn"], st["xnb"], 2, sc)
                ms = [sb(sc, "ms%d" % i, [128, D], F32) for i in range(2)]
                msb = [Buf("ms%d" % i) for i in range(2)]
                junk = sb(sc, "junk", [128, D], BF16)
                ss = sb(sc, "mss", [128, 2], F32)
                jb = Buf("junk")
                for mb in range(2):
                    kb.dma("sp", ms[mb][:], mem_d[mb * 128:(mb + 1) * 128, :], [], [msb[mb]], sem="ms%d" % mb)
                    kb.emit("act", lambda e: e.activation(out=junk[:], in_=ms[mb][:], func=AF.Square, accum_out=ss[:, mb:mb + 1]), [msb[mb]], [jb])
                    kb.emit("dve", lambda e: e.tensor_scalar(out=ss[:, mb:mb + 1], in0=ss[:, mb:mb + 1], scalar1=1.0 / D, scalar2=EPS, op0=ALU.mult, op1=ALU.add), [jb], [jb])
                    kb.emit("act", lambda e: e.activation(out=ss[:, mb:mb + 1], in_=ss[:, mb:mb + 1], func=AF.Sqrt), [jb], [jb])
                    kb.emit("dve", lambda e: e.reciprocal(out=ss[:, mb:mb + 1], in_=ss[:, mb:mb + 1]), [jb], [jb])
                    kb.emit("dve", lambda e: e.tensor_scalar(out=ms[mb][:], in0=ms[mb][:], scalar1=ss[:, mb:mb + 1], scalar2=None, op0=ALU.mult), [jb, msb[mb]], [msb[mb]])
                    for g4 in range(4):
                        ps, pb = kb.next_ps()
                        for i in range(4):
                            fc = g4 * 4 + i
                            kb.emit("pe", lambda e: e.transpose(ps[:, i * 128:(i + 1) * 128], ms[mb][:, fc * 128:(fc + 1) * 128], ident), [msb[mb], cstb], [pb])
                        for i in range(4):
                            fc = g4 * 4 + i
                            kb.emit("dve", lambda e: e.tensor_scalar(out=st["memnT"][:, fc, mb * 128:(mb + 1) * 128], in0=ps[:, i * 128:(i + 1) * 128],
                                                                      scalar1=pp[:, P_GMEM + fc:P_GMEM + fc + 1], scalar2=None, op0=ALU.mult), [pb, ppb], [st["memnTb"]])
                kb.barrier()
                sc.close()
                o = contextlib.ExitStack()
                st["xa_i"] = o
                st["oT"] = sb(o, "oT", [128, 16, T], BF16)
                st["oTb"] = [Buf("oT%d" % i) for i in range(16)]
                st["KT"] = sb(o, "KT", [128, 16, NMEM], BF16)
                st["Vx"] = sb(o, "Vx", [128, 2, D], BF16)
                st["qh"] = sb(o, "qh", [128, 4, T], BF16)
                st["PX"] = [sb(o, "PX%d" % i, [128, 512], BF16) for i in range(2)]
                st["PXb"] = [Buf("PX%d" % i) for i in range(2)]
                st["rdx"] = sb(o, "rdx", [128, 512], F32)
                for n in ("KT", "Vx", "qh", "rdx"):
                    st[n + "b"] = Buf(n)
            add_call(xa_open)

            def mem_rhs(k, tt):
                return st["memnT"][:, k, :], st["memnTb"]

            def mem_lhs(k, mb):
                return st["memnT"][:, k, mb * 128:(mb + 1) * 128], st["memnTb"]

            for nn in range(8):
                def h_wk(slot, slotb, nn=nn):
                    def ep(ci, tt, ps, pb):
                        act_copy(st["KT"][:, nn * 2 + ci, :], ps[:, 0:NMEM], [pb], [st["KTb"]])
                    proj_fm(slot, slotb, [(0, 128), (128, 128)], mem_rhs, 16, ep, ntt=1)
                add_job([(0, 16, 0, 256, wsrc(wk_d, l, 0, 16, nn * 256, 256))], h_wk)
            for nn in range(8):
                def h_wv(slot, slotb, nn=nn):
                    def ep(mb, ps, pb):
                        act_copy(st["Vx"][:, mb, nn * 256:(nn + 1) * 256], ps[:, 0:256], [pb], [st["Vxb"]])
                    proj_tm(slot, slotb, 0, 256, mem_lhs, 16, ep, ntb=2)
                add_job([(0, 16, 0, 256, wsrc(wv_d, l, 0, 16, nn * 256, 256))], h_wv)

            XSC = 512.0 ** -0.5
            for h in range(4):
                for half in range(2):
                    def h_wq(slot, slotb, h=h, half=half):
                        def ep(ci, tt, ps, pb):
                            act_copy(st["qh"][:, half * 2 + ci, tt * 512:(tt + 1) * 512], ps[:], [pb], [st["qhb"]])
                        proj_fm(slot, slotb, [(0, 128), (128, 128)], xn_rhs, 16, ep)
                        if half == 0:
                            return
                        for tt in range(2):
                            ts = slice(tt * 512, (tt + 1) * 512)
                            for mb in range(2):
                                pss, pbs = kb.next_ps()
                                for c in range(4):
                                    kb.mm(pss[:], st["KT"][:, 4 * h + c, mb * 128:(mb + 1) * 128], st["qh"][:, c, ts], c == 0, c == 3, [st["KTb"], st["qhb"]], [pbs], inc=(c == 3))
                                kb.emit("act", lambda e: e.activation(out=st["PX"][mb][:], in_=pss[:], func=AF.Exp, scale=XSC), [pbs], [st["PXb"][mb]])
                            psd, pbd = kb.next_ps()
                            for mb in range(2):
                                kb.mm(psd[:], ones_bf[:], st["PX"][mb][:], mb == 0, mb == 1, [cbf, st["PXb"][mb]], [pbd], inc=(mb == 1))
                            kb.emit("dve", lambda e: e.reciprocal(out=st["rdx"][:], in_=psd[:]), [pbd], [st["rdxb"]])
                            for c in range(4):
                                pso, pbo = kb.next_ps()
                                for mb in range(2):
                                    kb.mm(pso[:], st["Vx"][:, mb, (4 * h + c) * 128:(4 * h + c + 1) * 128], st["PX"][mb][:], mb == 0, mb == 1, [st["Vxb"], st["PXb"][mb]], [pbo], inc=(mb == 1))
                                kb.emit("dve", lambda e: e.tensor_tensor(out=st["oT"][:, 4 * h + c, ts], in0=pso[:], in1=st["rdx"][:], op=ALU.mult), [pbo, st["rdxb"]], [st["oTb"][4 * h + c]])
                    add_job([(0, 16, 0, 256, wsrc(wq_d, l, 0, 16, h * 512 + half * 256, 256))], h_wq)

            def oT_rhs(k, tt):
                return st["oT"][:, k, tt * 512:(tt + 1) * 512], st["oTb"][k]
            for nn in range(8):
                def h_xo(slot, slotb, nn=nn):
                    proj_fm(slot, slotb, [(0, 128), (128, 128)], oT_rhs, 16, resid_ep(nn * 2))
                add_job([(0, 16, 0, 256, wsrc(wo_d, l, 0, 16, nn * 256, 256))], h_xo)

            def xa_close():
                kb.barrier()
                st["xa_i"].close()
                st["xa_o"].close()
            add_call(xa_close)
            if stop_after == "xa" and l == NL - 1:
                return

            def ffn_open():
                o = contextlib.ExitStack()
                st["ffn_o"] = o
                st["xn"] = sb(o, "xn", [128, 16, T + 2], BF16)
                st["xnb"] = [Buf("xn%d" % i) for i in range(16)]
                st["act"] = sb(o, "actb", [128, 11, T], BF16)
                st["actb"] = [Buf("act%d" % i) for i in range(11)]
                st["ug"] = [sb(o, "ug%d" % i, [128, T + 2], F32) for i in range(2)]
                st["uv"] = [sb(o, "uv%d" % i, [128, T + 2], F32) for i in range(2)]
                st["ugb"] = [Buf("ug%d" % i) for i in range(2)]
                st["uvb"] = [Buf("uv%d" % i) for i in range(2)]
                st["cg"] = sb(o, "cg", [128, 512], F32)
                st["cv"] = sb(o, "cv", [128, 512], F32)
                st["sg"] = sb(o, "sg", [128, 512], F32)
                for n in ("cg", "cv", "sg"):
                    st[n + "b"] = Buf(n)
                sc = contextlib.ExitStack()
                rmsnorm(P_GFFN, st["xn"], st["xnb"], 2, sc)
                pay3 = sb(sc, "pay3", [128, 32], F32)
                p3b = Buf("pay3")
                kb.emit("dve", lambda e: e.tensor_copy(out=pay3[:].rearrange("p (k t) -> p k t", k=16), in_=st["xn"][:, :, T:T + 2]), st["xnb"], [p3b])
                kb.dma("sp", gin3.ap()[:, :], pay3[:], [p3b], [gb["gin3"]], sem="g3")
                allgather(gin3, gout3, gb["gin3"], gb["gout3"])
                hal3 = sb(sc, "hal3", [128, 8, 32], F32)
                h3b = Buf("hal3")
                kb.dma("sp", hal3[:], gout3.ap().rearrange("(r p) n -> p r n", p=128), [gb["gout3"]], [h3b], sem="g3l")
                acc3 = sb(sc, "acc3", [128, 32], F32)
                kb.emit("dve", lambda e: e.tensor_scalar(out=acc3[:], in0=hal3[:, 0, :], scalar1=cst[:, C_SEL:C_SEL + 1], scalar2=None, op0=ALU.mult), [h3b, cstb], [p3b])
                for r in range(1, 8):
                    kb.emit("dve", lambda e: e.scalar_tensor_tensor(out=acc3[:], in0=hal3[:, r, :], scalar=cst[:, C_SEL + r:C_SEL + r + 1], in1=acc3[:], op0=ALU.mult, op1=ALU.add), [h3b, cstb, p3b], [p3b])
                kb.emit("dve", lambda e: e.tensor_copy(out=st["xn"][:, :, 0:2], in_=acc3[:].rearrange("p (k t) -> p k t", k=16)), [p3b], st["xnb"])
                kb.barrier()
                sc.close()
            add_call(ffn_open)

            for grp in range(4):
                for i in range(11):
                    def h_up(slot, slotb, grp=grp, i=i):
                        c = grp * 11 + i
                        par = c % 2
                        for which, off, ut, utb, cc in (("g", 0, st["ug"][par], st["ugb"][par], c), ("v", 128, st["uv"][par], st["uvb"][par], 44 + c)):
                            for tt in range(2):
                                ps, pb = kb.next_ps()
                                for k in range(16):
                                    kb.mm(ps[:], slot[:, k, off:off + 128], st["xn"][:, k, 2 + tt * 512:2 + (tt + 1) * 512], k == 0, k == 15, [slotb, st["xnb"][k]], [pb], inc=(k == 15))
                                act_copy(ut[:, 2 + tt * 512:2 + (tt + 1) * 512], ps[:], [pb], [utb])
                            ps, pb = kb.next_ps()
                            for k in range(16):
                                kb.mm(ps[:, 0:2], slot[:, k, off:off + 128], st["xn"][:, k, 0:2], k == 0, k == 15, [slotb, st["xnb"][k]], [pb], inc=(k == 15))
                            act_copy(ut[:, 0:2], ps[:, 0:2], [pb], [utb])
                        ug, ugb, uv, uvb = st["ug"][par], st["ugb"][par], st["uv"][par], st["uvb"][par]
                        wc = lambda cc, tap: pp[:, P_FCONV + cc * 3 + tap:P_FCONV + cc * 3 + tap + 1]
                        bc = lambda cc: pp[:, P_FB + cc:P_FB + cc + 1]
                        for tt in range(2):
                            t0 = tt * 512
                            for (u_, ub_, cc, dst, dstb) in ((ug, ugb, c, st["cg"], st["cgb"]), (uv, uvb, 44 + c, st["cv"], st["cvb"])):
                                kb.emit("dve", lambda e: e.tensor_scalar(out=dst[:], in0=u_[:, t0:t0 + 512], scalar1=wc(cc, 0), scalar2=bc(cc), op0=ALU.mult, op1=ALU.add), [ub_, ppb], [dstb])
                                kb.emit("dve", lambda e: e.scalar_tensor_tensor(out=dst[:], in0=u_[:, t0 + 1:t0 + 513], scalar=wc(cc, 1), in1=dst[:], op0=ALU.mult, op1=ALU.add), [ub_, ppb, dstb], [dstb])
                                kb.emit("dve", lambda e: e.scalar_tensor_tensor(out=dst[:], in0=u_[:, t0 + 2:t0 + 514], scalar=wc(cc, 2), in1=dst[:], op0=ALU.mult, op1=ALU.add), [ub_, ppb, dstb], [dstb])
                            kb.emit("act", lambda e: e.activation(out=st["sg"][:], in_=st["cg"][:], func=AF.Silu), [st["cgb"]], [st["sgb"]])
                            kb.emit("dve", lambda e: e.tensor_tensor(out=st["act"][:, i, t0:t0 + 512], in0=st["sg"][:], in1=st["cv"][:], op=ALU.mult), [st["sgb"], st["cvb"]], [st["actb"][i]])
                    c = grp * 11 + i
                    add_job([(0, 16, 0, 128, wsrc(wup_d, l, 0, 16, c * 128, 128)), (0, 16, 128, 128, wsrc(wup_d, l, 0, 16, DFF + c * 128, 128))], h_up)

                def act_rhs(k, tt):
                    return st["act"][:, k, tt * 512:(tt + 1) * 512], st["actb"][k]
                for nn in range(8):
                    def h_dn(slot, slotb, nn=nn):
                        proj_fm(slot, slotb, [(0, 128), (128, 128)], act_rhs, 11, resid_ep(nn * 2))
                    add_job([(0, 11, 0, 256, wsrc(wdn_d, l, grp * 11, 11, nn * 256, 256))], h_dn)

            def ffn_close():
                kb.barrier()
                st["ffn_o"].close()
            add_call(ffn_close)

        for l in range(NL):
            layer_steps(l)

        def final():
            sc = contextlib.ExitStack()
            sq = [sb(sc, "fsq%d" % i, [128, 512], BF16) for i in range(2)]
            sqb = [Buf("fsq%d" % i) for i in range(2)]
            rs = sb(sc, "frs", [128, T], F32)
            rsb = Buf("frs")
            for tt in range(2):
                ts = slice(tt * 512, (tt + 1) * 512)
                ps, pb = kb.next_ps()
                for k in range(16):
                    kb.emit("act", lambda e: e.activation(out=sq[k % 2][:], in_=hT[:, k, ts], func=AF.Square), [hTb[k]], [sqb[k % 2]])
                    kb.mm(ps[:], ones_bf[:], sq[k % 2][:], k == 0, k == 15, [sqb[k % 2], cbf], [pb])
                kb.emit("dve", lambda e: e.tensor_scalar(out=rs[:, ts], in0=ps[:], scalar1=1.0 / D, scalar2=EPS, op0=ALU.mult, op1=ALU.add), [pb], [rsb])
                kb.emit("act", lambda e: e.activation(out=rs[:, ts], in_=rs[:, ts], func=AF.Sqrt), [rsb], [rsb])
                kb.emit("dve", lambda e: e.reciprocal(out=rs[:, ts], in_=rs[:, ts]), [rsb], [rsb])
            yn = [sb(sc, "yn%d" % i, [128, 4, 128], F32) for i in range(2)]
            ynb = [Buf("yn%d" % i) for i in range(2)]
            ot = [sb(sc, "ot%d" % i, [128, D], F32) for i in range(2)]
            otb = [Buf("ot%d" % i) for i in range(2)]
            n = 0
            for tb in range(8):
                cs = slice(tb * 128, (tb + 1) * 128)
                for g4 in range(4):
                    y, yb = yn[n % 2], ynb[n % 2]
                    n += 1
                    for i in range(4):
                        k = g4 * 4 + i
                        kb.emit("dve", lambda e: e.scalar_tensor_tensor(out=y[:, i, :], in0=hT[:, k, cs], scalar=cst[:, C_GFIN + k:C_GFIN + k + 1], in1=rs[:, cs], op0=ALU.mult, op1=ALU.mult),
                                [hTb[k], rsb, cstb], [yb])
                    ps, pb = kb.next_ps()
                    for i in range(4):
                        kb.emit("pe", lambda e: e.transpose(ps[:, i * 128:(i + 1) * 128], y[:, i, :], ident), [yb, cstb], [pb])
                    act_copy(ot[tb % 2][:, g4 * 512:(g4 + 1) * 512], ps[:], [pb], [otb[tb % 2]])
                kb.dma("sp", out_d[tb * 128:(tb + 1) * 128, :], ot[tb % 2][:], [otb[tb % 2]], [], sem="out%d" % (tb % 2))
            for i in range(2):
                h, c = kb.dsems["out%d" % i]
                nc.sync.wait_ge(h, c)
            if "dbg" in kb.dsems:
                h, c = kb.dsems["dbg"]
                nc.sync.wait_ge(h, c)
            kb.barrier()
            sc.close()

        if CUT is not None:
            steps = steps[:CUT]

            def cleanup():
                kb.barrier()
                for k_ in ("swa", "gla2", "gla", "xa_i", "ffn_o", "xa_o", "mx_o"):
                    if k_ in st:
                        st[k_].close()
            add_call(cleanup)
        add_call(final)
        norm_steps = []
        for s_ in steps:
            if s_[0] == "job":
                wstate["jobs"].append(s_[1])
                norm_steps.append(("job", s_[2]))
            else:
                norm_steps.append(s_)
        for s_ in norm_steps:
            if s_[0] == "job":
                slot, slotb = get_slot()
                s_[1](slot, slotb)
            else:
                s_[1]()
    return nc, dbg_map


def host_consts(core):
    c = np.zeros((128, NCST), np.float32)
    i = np.arange(128)
    c[:, C_IDENT:C_IDENT + 128] = np.eye(128, dtype=np.float32)
    c[:, C_TRI:C_TRI + 128] = np.where(i[:, None] <= i[None, :], -1.0 / 16.0, 0.0)
    c[:, C_LE:C_LE + 128] = (i[:, None] <= i[None, :]).astype(np.float32)
    c[:, C_GT:C_GT + 128] = (i[:, None] > i[None, :]).astype(np.float32)
    P = np.zeros((128, 128), np.float32)
    for hb in range(2):
        for dd in range(32):
            P[hb * 64 + dd, hb * 64 + dd + 32] = -1.0
            P[hb * 64 + dd + 32, hb * 64 + dd] = 1.0
    c[:, C_ROTP:C_ROTP + 128] = P.T
    c[:, C_ONES:C_ONES + 128] = 1.0
    inv = 1.0 / (10000.0 ** (np.arange(0, 64, 2, dtype=np.float32) / 64.0))
    c[:, C_INVF] = inv[(i % 64) % 32]
    if core > 0:
        c[:, C_SEL + core - 1] = 1.0
        c[:, C_HASPREV] = 1.0
    c[:, C_LT:C_LT + 8] = (np.arange(8) < core).astype(np.float32)[None, :]
    return c


SIM_HOOK = None


def _run(inputs, NL=DEPTH, dbg_spec=None, stop_after=None):
    f = lambda a: np.ascontiguousarray(np.asarray(a))
    x = f(inputs["x"])[0]
    mem = f(inputs["mem"])[0]
    pos = f(inputs["positions"])[0].astype(np.int32)
    pp = np.zeros((DEPTH, 128, NPP), np.float32)
    for l in range(DEPTH):
        for nm, off in (("norm_mix", P_GMIX), ("norm_x", P_GX), ("norm_mem", P_GMEM), ("norm_ffn", P_GFFN)):
            pp[l, :, off:off + 16] = f(inputs[nm])[l].reshape(16, 128).T
        pp[l, :, P_GLAN] = f(inputs["gla_norm"])[l]
        pp[l, :, P_SCONV:P_SCONV + 12] = f(inputs["sc_conv"])[l].reshape(3, 4, 128).transpose(2, 1, 0).reshape(128, 12)
        pp[l, :, P_FCONV:P_FCONV + 264] = f(inputs["ffn_conv"])[l].reshape(3, 88, 128).transpose(2, 1, 0).reshape(128, 264)
        pp[l, :, P_FB:P_FB + 88] = f(inputs["ffn_conv_b"])[l].reshape(88, 128).T
        sk = f(inputs["swa_sinks"])[l]
        pp[l, 0:64, P_SINK:P_SINK + 8] = sk[None, 0:8]
        pp[l, 64:128, P_SINK:P_SINK + 8] = sk[None, 8:16]
    wg = np.concatenate([f(inputs["gla_w_gate"]), f(inputs["gla_b_gate"])[:, None, :]], axis=1).astype(np.float32)
    gfin = f(inputs["norm_final"]).reshape(16, 128).T
    shared = {k: f(inputs[k])[:NL] for k in needed_weights(NL, stop_after)}
    in_maps = []
    for c in range(NCORE):
        cst = host_consts(c)
        cst[:, C_GFIN:C_GFIN + 16] = gfin
        m = dict(shared)
        m.update({"x": np.ascontiguousarray(x[c * T:(c + 1) * T]), "mem": mem,
                  "pos": np.ascontiguousarray(np.broadcast_to(pos[c * T:(c + 1) * T][None, :], (128, T))),
                  "cst": cst, "pp": pp, "wg": wg})
        in_maps.append(m)
    nc, dbg_map = build(NL, dbg_spec, stop_after)
    if SIM_HOOK is not None:
        return SIM_HOOK(nc, in_maps), None, dbg_map
    res = run_bass_kernel_spmd(nc, in_maps, core_ids=list(range(NCORE)))
    out = np.concatenate([res.results[c]["out"] for c in range(NCORE)], axis=0)[None]
    return out.astype(np.float32), res, dbg_map


def kernel(**inputs):
    out, _, _ = _run(inputs)
    return out
```
